# Optimizing a Trainium2 kernel written in Bass

```python
import math
import jax, jax.numpy as jnp
from jax import lax
import numpy as np

D_MODEL = 1024
BATCH = 32
SEQ = 2048
DEPTH = 1

N_META = 16
SSM_WIDTH = 1024
SSM_GROUP = 16
SSM_GROUPS = SSM_WIDTH // SSM_GROUP
SSM_STATE = 64
DT_MIN = 1e-3
DT_MAX = 1e-1
HGRN_WIDTH = 1024
HGRN_HEAD_DIM = 128
HGRN_HEADS = HGRN_WIDTH // HGRN_HEAD_DIM
HGRN_CHUNK = 16
D_FF = 2816
CONV_WIDTH = 3
EPS = 1e-6
IN_COLS = SSM_WIDTH + 4 * HGRN_WIDTH + 2 * D_MODEL

kernel_name = 'hybrid_s5_hgrn2_gated_merge_block'


def rmsnorm(x, g):
    xf = x.astype(jnp.float32)
    y = xf * lax.rsqrt(jnp.mean(xf * xf, axis=-1, keepdims=True) + EPS)
    return (y * g.astype(jnp.float32)).astype(x.dtype)


def _complex_affine_combine(e1, e2):
    a1r, a1i, b1r, b1i = e1
    a2r, a2i, b2r, b2i = e2
    ar = a1r * a2r - a1i * a2i
    ai = a1r * a2i + a1i * a2r
    br = a2r * b1r - a2i * b1i + b2r
    bi = a2r * b1i + a2i * b1r + b2i
    return ar, ai, br, bi


def s5_mixer(u, lam_re, lam_im, log_dt, b_re, b_im, c_re, c_im, d_skip, w_glu):
    bsz, L, _ = u.shape
    uf = u.astype(jnp.float32).reshape(bsz, L, SSM_GROUPS, SSM_GROUP)
    lr = lam_re.astype(jnp.float32)
    li = lam_im.astype(jnp.float32)
    dt = jnp.exp(log_dt.astype(jnp.float32))[:, None]
    mag = jnp.exp(lr * dt)
    ab_re = mag * jnp.cos(li * dt)
    ab_im = mag * jnp.sin(li * dt)
    den = lr * lr + li * li
    nr = ab_re - 1.0
    coef_re = (nr * lr + ab_im * li) / den
    coef_im = (ab_im * lr - nr * li) / den
    br = b_re.astype(jnp.float32)
    bi = b_im.astype(jnp.float32)
    bb_re = coef_re[..., None] * br - coef_im[..., None] * bi
    bb_im = coef_re[..., None] * bi + coef_im[..., None] * br
    v_re = jnp.einsum('blgh,gph->blgp', uf, bb_re)
    v_im = jnp.einsum('blgh,gph->blgp', uf, bb_im)
    a_re = jnp.broadcast_to(ab_re[None, None], (1, L, SSM_GROUPS, SSM_STATE))
    a_im = jnp.broadcast_to(ab_im[None, None], (1, L, SSM_GROUPS, SSM_STATE))
    _, _, s_re, s_im = lax.associative_scan(_complex_affine_combine, (a_re, a_im, v_re, v_im), axis=1)
    y = (jnp.einsum('blgp,ghp->blgh', s_re, c_re.astype(jnp.float32))
         - jnp.einsum('blgp,ghp->blgh', s_im, c_im.astype(jnp.float32))
         + d_skip.astype(jnp.float32).reshape(SSM_GROUPS, SSM_GROUP) * uf)
    y = jax.nn.gelu(y.reshape(bsz, L, SSM_WIDTH)).astype(u.dtype)
    return y * jax.nn.sigmoid(y @ w_glu)


def hgrn2_mixer(q, f_logit, i_in, og, lb, norm_g):
    bsz, L, _ = q.shape
    n_chunks = L // HGRN_CHUNK

    def heads(t):
        return t.reshape(bsz, n_chunks, HGRN_CHUNK, HGRN_HEADS, HGRN_HEAD_DIM).transpose(1, 0, 3, 2, 4)

    lbf = lb.astype(jnp.float32)
    f = lbf + (1.0 - lbf) * jax.nn.sigmoid(f_logit.astype(jnp.float32))
    log_f = jnp.log(f)
    qh = heads(q.astype(jnp.float32))
    kh = heads(1.0 - f)
    vh = heads(i_in.astype(jnp.float32))
    cum = jnp.cumsum(heads(log_f), axis=3)
    q_in = qh * jnp.exp(cum)
    k_in = kh * jnp.exp(-cum)
    k_out = kh * jnp.exp(cum[..., -1:, :] - cum)
    chunk_decay = jnp.exp(cum[..., -1, :])
    causal = jnp.tril(jnp.ones((HGRN_CHUNK, HGRN_CHUNK), dtype=bool))

    def chunk_step(state, xs):
        qi, ki, ko, v, dec = xs
        scores = jnp.where(causal, jnp.einsum('bhtd,bhsd->bhts', qi, ki), 0.0)
        o = jnp.einsum('bhts,bhsv->bhtv', scores, v) + jnp.einsum('bhtd,bhdv->bhtv', qi, state)
        state = dec[..., None] * state + jnp.einsum('bhsd,bhsv->bhdv', ko, v)
        return state, o

    init = jnp.zeros((bsz, HGRN_HEADS, HGRN_HEAD_DIM, HGRN_HEAD_DIM), jnp.float32)
    _, o = lax.scan(chunk_step, init, (q_in, k_in, k_out, vh, chunk_decay))
    o = o.transpose(1, 0, 3, 2, 4).reshape(bsz, L, HGRN_HEADS, HGRN_HEAD_DIM)
    o = o * lax.rsqrt(jnp.mean(o * o, axis=-1, keepdims=True) + EPS) * norm_g.astype(jnp.float32)
    o = o.reshape(bsz, L, HGRN_WIDTH).astype(q.dtype)
    return o * jax.nn.silu(og)


def causal_dwconv(u, w, b):
    L = u.shape[1]
    up = jnp.pad(u, ((0, 0), (CONV_WIDTH - 1, 0), (0, 0)))
    out = b
    for j in range(CONV_WIDTH):
        out = out + up[:, j:j + L, :] * w[j]
    return out


def setup_inputs(seed: int = 0) -> dict:
    key = jax.random.key(seed)
    ks = jax.random.split(key, 24)
    f32 = jnp.float32

    def nrm(k, shape, scale):
        return jax.random.normal(k, shape, f32) * scale

    n_idx = jnp.arange(SSM_STATE, dtype=f32)
    return {
        'x': nrm(ks[0], (BATCH, SEQ, D_MODEL), 1.0),
        'meta_tokens': nrm(ks[1], (N_META, D_MODEL), 1.0),
        'mix_norm_g': 1.0 + nrm(ks[2], (DEPTH, D_MODEL), 0.02),
        'w_in': nrm(ks[3], (DEPTH, D_MODEL, IN_COLS), D_MODEL ** -0.5),
        'ssm_lambda_re': -0.5 + nrm(ks[4], (DEPTH, SSM_GROUPS, SSM_STATE), 0.01),
        'ssm_lambda_im': math.pi * n_idx + nrm(ks[5], (DEPTH, SSM_GROUPS, SSM_STATE), 0.01),
        'ssm_log_dt': jax.random.uniform(ks[6], (DEPTH, SSM_GROUPS), f32, math.log(DT_MIN), math.log(DT_MAX)),
        'ssm_b_re': nrm(ks[7], (DEPTH, SSM_GROUPS, SSM_STATE, SSM_GROUP), (2 * SSM_GROUP) ** -0.5),
        'ssm_b_im': nrm(ks[8], (DEPTH, SSM_GROUPS, SSM_STATE, SSM_GROUP), (2 * SSM_GROUP) ** -0.5),
        'ssm_c_re': nrm(ks[9], (DEPTH, SSM_GROUPS, SSM_GROUP, SSM_STATE), SSM_STATE ** -0.5),
        'ssm_c_im': nrm(ks[10], (DEPTH, SSM_GROUPS, SSM_GROUP, SSM_STATE), SSM_STATE ** -0.5),
        'ssm_d': nrm(ks[11], (DEPTH, SSM_WIDTH), 1.0),
        'ssm_w_glu': nrm(ks[12], (DEPTH, SSM_WIDTH, SSM_WIDTH), SSM_WIDTH ** -0.5),
        'w_ssm_proj': nrm(ks[13], (DEPTH, SSM_WIDTH, D_MODEL), SSM_WIDTH ** -0.5),
        'hgrn_lb_logits': nrm(ks[14], (DEPTH + 1, HGRN_WIDTH), 0.1),
        'hgrn_norm_g': 1.0 + nrm(ks[15], (DEPTH, HGRN_HEAD_DIM), 0.02),
        'w_hgrn_proj': nrm(ks[16], (DEPTH, HGRN_WIDTH, D_MODEL), HGRN_WIDTH ** -0.5),
        'w_out': nrm(ks[17], (DEPTH, D_MODEL, D_MODEL), D_MODEL ** -0.5),
        'ffn_norm_g': 1.0 + nrm(ks[18], (DEPTH, D_MODEL), 0.02),
        'w_up': nrm(ks[19], (DEPTH, D_MODEL, 2 * D_FF), D_MODEL ** -0.5),
        'conv_w': nrm(ks[20], (DEPTH, CONV_WIDTH, 2 * D_FF), CONV_WIDTH ** -0.5),
        'conv_b': nrm(ks[21], (DEPTH, 2 * D_FF), 0.01),
        'w_down': nrm(ks[22], (DEPTH, D_FF, D_MODEL), D_FF ** -0.5),
        'final_norm_g': 1.0 + nrm(ks[23], (D_MODEL,), 0.02),
    }


def reference(x, meta_tokens, mix_norm_g, w_in, ssm_lambda_re, ssm_lambda_im, ssm_log_dt,
              ssm_b_re, ssm_b_im, ssm_c_re, ssm_c_im, ssm_d, ssm_w_glu, w_ssm_proj,
              hgrn_lb_logits, hgrn_norm_g, w_hgrn_proj, w_out, ffn_norm_g, w_up, conv_w,
              conv_b, w_down, final_norm_g):
    bsz = x.shape[0]
    meta = jnp.broadcast_to(meta_tokens.astype(x.dtype)[None], (bsz, N_META, D_MODEL))
    h = jnp.concatenate([meta, x], axis=1)
    lower_bounds = jnp.cumsum(jax.nn.softmax(hgrn_lb_logits.astype(jnp.float32), axis=0), axis=0)
    o1 = SSM_WIDTH
    o2 = o1 + HGRN_WIDTH
    o3 = o2 + HGRN_WIDTH
    o4 = o3 + HGRN_WIDTH
    o5 = o4 + HGRN_WIDTH
    o6 = o5 + D_MODEL
    for l in range(DEPTH):
        z = rmsnorm(h, mix_norm_g[l])
        p = z @ w_in[l]
        y_a = s5_mixer(p[..., :o1], ssm_lambda_re[l], ssm_lambda_im[l], ssm_log_dt[l],
                       ssm_b_re[l], ssm_b_im[l], ssm_c_re[l], ssm_c_im[l], ssm_d[l], ssm_w_glu[l])
        y_b = hgrn2_mixer(p[..., o1:o2], p[..., o2:o3], p[..., o3:o4], p[..., o4:o5],
                          lower_bounds[l], hgrn_norm_g[l])
        merged = (jax.nn.sigmoid(p[..., o5:o6]) * (y_a @ w_ssm_proj[l])
                  + jax.nn.sigmoid(p[..., o6:]) * (y_b @ w_hgrn_proj[l]))
        h = h + merged @ w_out[l]
        z = rmsnorm(h, ffn_norm_g[l])
        u = causal_dwconv(z @ w_up[l], conv_w[l], conv_b[l])
        h = h + (jax.nn.silu(u[..., :D_FF]) * u[..., D_FF:]) @ w_down[l]
    return rmsnorm(h[:, N_META:], final_norm_g)
```

```python
import math
import numpy as np
import concourse.bass as bass
import concourse.mybir as mybir
from concourse.bass_utils import run_bass_kernel_spmd

F32 = mybir.dt.float32
BF16 = mybir.dt.bfloat16
ALU = mybir.AluOpType
AF = mybir.ActivationFunctionType

NCORES = 8
SPC = 4
D = 1024
SEQ = 2048
NMETA = 16
TT = 512
DFF = 2816
EPS = 1e-6
NWB = 5
NWT = 47
SCAN_ENG = "dve"
USE_GELU_LUT = False

ENGS = ("pe", "dve", "act", "pool", "sp")


class StopBuild(Exception):
    pass


class Instr:
    __slots__ = ("eng", "fn", "deps", "pos", "marked", "sem", "val", "dma")

    def __init__(self, eng, fn):
        self.eng = eng
        self.fn = fn
        self.deps = set()
        self.marked = False
        self.sem = None
        self.val = 0
        self.dma = False
        self.pos = 0


class Prog:
    WIN = 3

    def __init__(self):
        self.streams = {e: [] for e in ENGS}
        self.res = {}
        self.dma_counts = {}
        self.last_dma = {}
        self.group_sems = set()

    def op(self, eng, fn, reads=(), writes=(), dma_sem=None, extra_deps=(), group=False):
        ins = Instr(eng, fn)
        deps = set(extra_deps)
        if eng != "pe":
            psr = [r for r in reads if isinstance(r, tuple) and r[0] in ("ps", "pst")]
            if psr:
                reads = [r for r in reads if r not in psr]
                writes = list(writes) + psr
        for r in reads:
            st = self.res.get(r)
            if st is not None and st[0] is not None:
                deps.add(st[0])
        for r in writes:
            st = self.res.get(r)
            if st is not None:
                if st[0] is not None:
                    deps.add(st[0])
                deps.update(st[1])
        for r in reads:
            st = self.res.get(r)
            if st is None:
                self.res[r] = (None, [ins])
            else:
                st[1].append(ins)
        for r in writes:
            self.res[r] = (ins, [])
        deps.discard(ins)
        ins.deps = deps
        ins.pos = len(self.streams[eng])
        if dma_sem is not None:
            ins.dma = True
            ins.marked = True
            ins.sem = dma_sem
            c = self.dma_counts.get(id(dma_sem), 0) + 16
            self.dma_counts[id(dma_sem)] = c
            ins.val = c
            self.last_dma[id(dma_sem)] = ins
            if group:
                self.group_sems.add(id(dma_sem))
        self.streams[eng].append(ins)
        return ins

    def fence(self):
        lasts = [self.streams[e][-1] for e in ENGS if self.streams[e]]
        lasts += list(self.last_dma.values())
        for e in ENGS:
            self.op(e, lambda eng: eng.nop(), extra_deps=lasts)
        self.res = {}

    def _skip(self, ins, d):
        if d.dma or ins.dma:
            return False
        if d.eng != ins.eng:
            return False
        if d.eng == "pe":
            return True
        return ins.pos - d.pos > self.WIN

    def finalize(self, eng_sems):
        for e in ENGS:
            for ins in self.streams[e]:
                if ins.dma and id(ins.sem) in self.group_sems:
                    ins.val = self.dma_counts[id(ins.sem)]
        for e in ENGS:
            for ins in self.streams[e]:
                for d in ins.deps:
                    if d.dma or self._skip(ins, d):
                        continue
                    d.marked = True
        for e in ENGS:
            c = 0
            for ins in self.streams[e]:
                if ins.dma:
                    continue
                ins.sem = eng_sems[e]
                if ins.marked:
                    c += 1
                    ins.val = c

    def emit(self, eng_name, eng):
        waited = {}
        for ins in self.streams[eng_name]:
            need = {}
            for d in ins.deps:
                if not d.marked or self._skip(ins, d):
                    continue
                k = id(d.sem)
                if k not in need or need[k][1] < d.val:
                    need[k] = (d.sem, d.val)
            for k, (sem, val) in need.items():
                if waited.get(k, 0) < val:
                    eng.wait_ge(sem, val)
                    waited[k] = val
            bi = ins.fn(eng)
            if ins.marked:
                bi.then_inc(ins.sem, 16 if ins.dma else 1)


VEC = {}
_o = 0
for _n, _w in (("g1", 8), ("g2", 8), ("g3", 8), ("lbl0", 8), ("lbl1", 8), ("ng", 1), ("d", 8),
               ("cw0", 44), ("cw1", 44), ("cw2", 44), ("cb", 44), ("mrow", 2), ("mpp", 2),
               ("eps", 1), ("hpi", 1), ("one", 1)):
    VEC[_n] = _o
    _o += _w
NV = _o


def _wtile(wsub):
    nk = wsub.shape[0] // 128
    t = np.zeros((128, 8, 512), np.float32)
    t[:, :nk, :] = wsub.reshape(nk, 128, 512).transpose(1, 0, 2)
    return t.reshape(128, 4096)


def _chunked(v):
    return np.ascontiguousarray(v.reshape(-1, 128).T)


def pack_inputs(inp):
    f = np.float32
    w_in = np.asarray(inp["w_in"], f)[0]
    tiles = []
    offs = {"u": 0, "q": 1024, "f": 2048, "i": 3072, "og": 4096, "ga": 5120, "gb": 6144}
    for s in ("u", "f", "q", "i", "og", "ga", "gb"):
        for h in range(2):
            tiles.append(_wtile(w_in[:, offs[s] + h * 512: offs[s] + (h + 1) * 512]))
    for name in ("ssm_w_glu", "w_ssm_proj", "w_hgrn_proj", "w_out"):
        w = np.asarray(inp[name], f)[0]
        for h in range(2):
            tiles.append(_wtile(w[:, h * 512:(h + 1) * 512]))
    w_up = np.asarray(inp["w_up"], f)[0]
    for i in range(11):
        cols = np.concatenate([np.arange(2 * i * 128, 2 * i * 128 + 128), DFF + np.arange(2 * i * 128, 2 * i * 128 + 128),
                               np.arange((2 * i + 1) * 128, (2 * i + 1) * 128 + 128),
                               DFF + np.arange((2 * i + 1) * 128, (2 * i + 1) * 128 + 128)])
        tiles.append(_wtile(w_up[:, cols]))
    w_down = np.asarray(inp["w_down"], f)[0]
    for r in range(3):
        rows = slice(r * 1024, min((r + 1) * 1024, DFF))
        for ch in range(2):
            tiles.append(_wtile(w_down[rows, ch * 512:(ch + 1) * 512]))
    wf = np.stack(tiles, 0)
    assert wf.shape[0] == 39

    vec = np.zeros((128, NV), f)
    vec[:, VEC["g1"]:VEC["g1"] + 8] = _chunked(np.asarray(inp["mix_norm_g"], f)[0])
    vec[:, VEC["g2"]:VEC["g2"] + 8] = _chunked(np.asarray(inp["ffn_norm_g"], f)[0])
    vec[:, VEC["g3"]:VEC["g3"] + 8] = _chunked(np.asarray(inp["final_norm_g"], f))
    vec[:, VEC["lbl0"]:VEC["lbl0"] + 8] = _chunked(np.asarray(inp["hgrn_lb_logits"], f)[0])
    vec[:, VEC["lbl1"]:VEC["lbl1"] + 8] = _chunked(np.asarray(inp["hgrn_lb_logits"], f)[1])
    vec[:, VEC["ng"]] = np.asarray(inp["hgrn_norm_g"], f)[0]
    vec[:, VEC["d"]:VEC["d"] + 8] = _chunked(np.asarray(inp["ssm_d"], f)[0])
    cw = np.asarray(inp["conv_w"], f)[0]
    for j in range(3):
        vec[:, VEC["cw%d" % j]:VEC["cw%d" % j] + 44] = _chunked(cw[j])
    vec[:, VEC["cb"]:VEC["cb"] + 44] = _chunked(np.asarray(inp["conv_b"], f)[0])
    p = np.arange(128)
    for e in range(2):
        vec[:, VEC["mrow"] + e] = ((p // 16) % 2 == e)
        vec[:, VEC["mpp"] + e] = ((p // 64) == e)
    vec[:, VEC["eps"]] = EPS
    vec[:, VEC["hpi"]] = math.pi / 2
    vec[:, VEC["one"]] = 1.0

    lre = np.asarray(inp["ssm_lambda_re"], f)[0]
    lim = np.asarray(inp["ssm_lambda_im"], f)[0]
    ldt = np.asarray(inp["ssm_log_dt"], f)[0]
    bre = np.asarray(inp["ssm_b_re"], f)[0]
    bim = np.asarray(inp["ssm_b_im"], f)[0]
    cre = np.asarray(inp["ssm_c_re"], f)[0]
    cim = np.asarray(inp["ssm_c_im"], f)[0]
    s5rows = np.zeros((128, 5, 8, 64), f)
    gl = p // 16
    hh = p % 16
    for fc in range(8):
        g = 8 * fc + gl
        s5rows[:, 0, fc, :] = lre[g, :]
        s5rows[:, 1, fc, :] = lim[g, :]
        s5rows[:, 2, fc, :] = ldt[g][:, None]
        s5rows[:, 3, fc, :] = bre[g, :, hh]
        s5rows[:, 4, fc, :] = bim[g, :, hh]
    s5rows = s5rows.reshape(128, 5 * 512)
    e_ = p // 64
    pp = p % 64
    s5pp = np.zeros((128, 96 + 4 * 512), f)
    for q in range(32):
        g = 2 * q + e_
        s5pp[:, q] = lre[g, pp]
        s5pp[:, 32 + q] = lim[g, pp]
        s5pp[:, 64 + q] = ldt[g]
        s5pp[:, 96 + q * 16: 96 + q * 16 + 16] = bre[g, pp, :]
        s5pp[:, 96 + 512 + q * 16: 96 + 512 + q * 16 + 16] = bim[g, pp, :]
        s5pp[:, 96 + 1024 + q * 16: 96 + 1024 + q * 16 + 16] = cre[g, :, pp]
        s5pp[:, 96 + 1536 + q * 16: 96 + 1536 + q * 16 + 16] = cim[g, :, pp]

    cst = np.zeros((128, 256), f)
    cst[:, 0:128] = np.eye(128, dtype=f)
    cst[:, 128:256] = np.triu(np.ones((128, 128), f))

    meta = np.asarray(inp["meta_tokens"], f)
    metaT = np.ascontiguousarray(meta.T.reshape(8, 128, NMETA).transpose(1, 0, 2))
    return dict(wf=wf, vec=vec, s5rows=s5rows, s5pp=s5pp, cst=cst, metaT=metaT)


def build_program(n_seq=SPC, n_tiles_per_seq=SEQ // TT, dbg=None, stop_after=None, ncast=39):
    nc = bass.Bass("TRN2", target_bir_lowering=False)
    xT = nc.dram_tensor("xT", [SPC, 128, 8, SEQ], F32, kind="ExternalInput").ap()
    metaT = nc.dram_tensor("metaT", [128, 8, NMETA], F32, kind="ExternalInput").ap()
    wf = nc.dram_tensor("wf", [39, 128, 4096], F32, kind="ExternalInput").ap()
    vec_d = nc.dram_tensor("vec", [128, NV], F32, kind="ExternalInput").ap()
    s5rows_d = nc.dram_tensor("s5rows", [128, 5 * 512], F32, kind="ExternalInput").ap()
    s5pp_d = nc.dram_tensor("s5pp", [128, 96 + 2048], F32, kind="ExternalInput").ap()
    cst_d = nc.dram_tensor("cst", [128, 256], F32, kind="ExternalInput").ap()
    outT = nc.dram_tensor("outT", [SPC, 128, 8, SEQ], F32, kind="ExternalOutput").ap()
    wb = nc.dram_tensor("wb", [NWT, 128, 4096], BF16, kind="Internal").ap()
    dbg_out = {}
    if dbg:
        for name, shape in dbg.items():
            if name.startswith("_"):
                continue
            dbg_out[name] = nc.dram_tensor("dbg_" + name, list(shape), F32, kind="ExternalOutput").ap()

    P = Prog()
    ARENA_WORDS = 53200

    class Arena:
        def __init__(self, base_ap):
            self.base = base_ap
            self.off = 0

        def reset(self, off=0):
            self.off = off

        def f32(self, n, shape=None):
            a = self.base[:, self.off:self.off + n]
            self.off += n
            assert self.off <= ARENA_WORDS, ("arena overflow", self.off)
            return a

        def bf16(self, n):
            w = (n + 1) // 2
            a = self.base[:, self.off:self.off + w].bitcast(BF16)
            self.off += w
            assert self.off <= ARENA_WORDS, ("arena overflow", self.off)
            return a

    import contextlib
    with contextlib.ExitStack() as es:
        big = es.enter_context(nc.sbuf_tensor("big", [128, ARENA_WORDS], F32))
        ps = es.enter_context(nc.psum_tensor("ps", [128, 7, 512], F32))
        pst = es.enter_context(nc.psum_tensor("pst", [128, 1024], BF16))
        sems = {e: es.enter_context(nc.semaphore("s_" + e)) for e in ENGS}
        sem_w = [es.enter_context(nc.semaphore("w%d" % i)) for i in range(NWB)]
        sem_x = [es.enter_context(nc.semaphore("x%d" % i)) for i in range(2)]
        sem_o = [es.enter_context(nc.semaphore("o%d" % i)) for i in range(2)]
        sem_c = [es.enter_context(nc.semaphore("c%d" % i)) for i in range(4)]
        sem_cv = [es.enter_context(nc.semaphore("cv%d" % i)) for i in range(4)]
        sem_s5 = [es.enter_context(nc.semaphore("s5w%d" % i)) for i in range(2)]
        sem_dbg = [es.enter_context(nc.semaphore("dbgs%d" % i)) for i in range(6)] if dbg else []
        dbg_n = [0]
        block = es.enter_context(nc.Block())

        A = Arena(big[:])
        vec = A.f32(NV)
        cst = A.f32(256)
        ident_bf = A.bf16(128)
        ones_bf = A.bf16(128)
        maskT = cst[:, 128:256]
        scanmsk = A.f32(512)
        lbv = A.f32(24).rearrange("p (a b) -> p a b", a=3)
        ar8 = A.f32(64)
        ai8 = A.f32(64)
        kall = A.bf16(8 * 8 * 128).rearrange("p (f k m) -> p f k m", f=8, k=8)
        RES_END = A.off

        def V(name, i=0, w=1):
            return vec[:, VEC[name] + i: VEC[name] + i + w]

        def pe_mm(out, lhsT, rhs, start, stop, reads, writes, tp=None):
            if tp is None:
                return P.op("pe", lambda e: e.matmul(out, lhsT, rhs, start=start, stop=stop), reads=reads, writes=writes)
            return P.op("pe", lambda e: e.matmul(out, lhsT, rhs, start=start, stop=stop, tile_position=tp),
                        reads=reads, writes=writes)

        def act(out, in_, func, reads, writes, bias=None, scale=1.0):
            if bias is None:
                return P.op("act", lambda e: e.activation(out, in_, func, scale=scale), reads=reads, writes=writes)
            return P.op("act", lambda e: e.activation(out, in_, func, bias=bias, scale=scale), reads=reads, writes=writes)

        def tt(eng, out, in0, in1, op, reads, writes):
            return P.op(eng, lambda e: e.tensor_tensor(out, in0, in1, op), reads=reads, writes=writes)

        def ts(eng, out, in0, s1, s2, op0, op1, reads, writes):
            return P.op(eng, lambda e: e.tensor_scalar(out, in0, s1, s2, op0, op1), reads=reads, writes=writes)

        def ts1(eng, out, in0, s1, op0, reads, writes):
            return P.op(eng, lambda e: e.tensor_single_scalar(out, in0, s1, op0), reads=reads, writes=writes)

        def stt(eng, out, in0, scalar, in1, op0, op1, reads, writes):
            return P.op(eng, lambda e: e.scalar_tensor_tensor(out, in0, scalar, in1, op0, op1), reads=reads, writes=writes)

        def cp(eng, out, in_, reads, writes):
            if eng == "act":
                return P.op("act", lambda e: e.copy(out, in_), reads=reads, writes=writes)
            return P.op(eng, lambda e: e.tensor_copy(out, in_), reads=reads, writes=writes)

        def dma(eng, out, in_, sem, reads, writes, group=False):
            return P.op(eng, lambda e: e.dma_start(out=out, in_=in_), reads=reads, writes=writes, dma_sem=sem, group=group)

        def dbg_dump(name, ap_sb, reads):
            if name in dbg_out:
                dma("pool", dbg_out[name], ap_sb, sem_dbg[dbg_n[0]], reads, [("dbg", name)])
                dbg_n[0] += 1

        dma("sp", vec, vec_d, sem_c[0], [], ["vec"])
        dma("sp", cst, cst_d, sem_c[1], [], ["cst"])
        for t in range(ncast):
            dma("pool", wb[t], wf[t], sem_cv[t % 4], [], [("wb", t)])
        cp("dve", ident_bf, cst[:, 0:128], ["cst"], ["ident"])
        P.op("dve", lambda e: e.memset(ones_bf, 1.0), writes=["ones"])
        P.op("dve", lambda e: e.memset(scanmsk, 1.0), writes=["scanmsk"])
        P.op("dve", lambda e: e.memset(scanmsk.rearrange("p (c j) -> p c j", j=128)[:, :, 0:1], 0.0),
             reads=["scanmsk"], writes=["scanmsk"])
        tt("dve", lbv[:, 0, :], V("lbl0", 0, 8), V("lbl1", 0, 8), ALU.subtract, ["vec"], ["lbv"])
        act(lbv[:, 0, :], lbv[:, 0, :], AF.Sigmoid, ["lbv"], ["lbv"])
        ts("dve", lbv[:, 1, :], lbv[:, 0, :], -1.0, 1.0, ALU.mult, ALU.add, ["lbv"], ["lbv"])
        ts1("dve", lbv[:, 2, :], lbv[:, 1, :], -1.0, ALU.mult, ["lbv"], ["lbv"])

        A.reset(RES_END)

        def coef_chain(lre, lim, ldt, F, alloc, key, nw, na, on_w, on_a):
            R = [key]
            dt = alloc(F); x = alloc(F); th = alloc(F); mag = alloc(F); c = alloc(F); s_ = alloc(F)
            t1 = alloc(F); t2 = alloc(F); t3 = alloc(F)
            act(dt, ldt, AF.Exp, R, R)
            tt("dve", x, lre, dt, ALU.mult, R, R)
            tt("dve", th, lim, dt, ALU.mult, R, R)
            act(mag, x, AF.Exp, R, R)
            ts1("dve", x, th, 1.0 / 16.0, ALU.mult, R, R)
            tt("dve", t1, x, x, ALU.mult, R, R)
            sc_ = [1.0 / 6227020800.0, -1.0 / 39916800.0, 1.0 / 362880.0, -1.0 / 5040.0, 1.0 / 120.0, -1.0 / 6.0]
            ts1("dve", t2, t1, sc_[0], ALU.mult, R, R)
            for cf in sc_[1:]:
                stt("dve", t2, t2, cf, t1, ALU.add, ALU.mult, R, R)
            stt("dve", s_, t2, 1.0, x, ALU.add, ALU.mult, R, R)
            cc_ = [-1.0 / 87178291200.0, 1.0 / 479001600.0, -1.0 / 3628800.0, 1.0 / 40320.0, -1.0 / 720.0,
                   1.0 / 24.0, -0.5]
            ts1("dve", t3, t1, cc_[0], ALU.mult, R, R)
            for cf in cc_[1:]:
                stt("dve", t3, t3, cf, t1, ALU.add, ALU.mult, R, R)
            ts1("dve", c, t3, 1.0, ALU.add, R, R)
            for _ in range(4):
                tt("dve", t1, c, c, ALU.mult, R, R)
                tt("dve", t2, s_, s_, ALU.mult, R, R)
                tt("dve", t3, c, s_, ALU.mult, R, R)
                tt("dve", c, t1, t2, ALU.subtract, R, R)
                ts1("dve", s_, t3, 2.0, ALU.mult, R, R)
            tt("dve", t1, c, c, ALU.mult, R, R)
            tt("dve", t2, s_, s_, ALU.mult, R, R)
            tt("dve", t1, t1, t2, ALU.add, R, R)
            ts("dve", t1, t1, -0.5, 1.5, ALU.mult, ALU.add, R, R)
            tt("dve", c, c, t1, ALU.mult, R, R)
            tt("dve", s_, s_, t1, ALU.mult, R, R)
            ar = alloc(F); ai = alloc(F)
            tt("dve", ar, mag, c, ALU.mult, R, R)
            tt("dve", ai, mag, s_, ALU.mult, R, R)
            den = dt
            tt("dve", t1, lre, lre, ALU.mult, R, R)
            tt("dve", t2, lim, lim, ALU.mult, R, R)
            tt("dve", den, t1, t2, ALU.add, R, R)
            P.op("dve", lambda e: e.reciprocal(den, den), reads=R, writes=R)
            nr = mag
            ts1("dve", nr, ar, -1.0, ALU.add, R, R)
            w0r = alloc(F); w0i = alloc(F); w1r = alloc(F); w1i = alloc(F)
            tt("dve", t1, nr, lre, ALU.mult, R, R)
            tt("dve", t2, ai, lim, ALU.mult, R, R)
            tt("dve", t1, t1, t2, ALU.add, R, R)
            tt("dve", w0r, t1, den, ALU.mult, R, R)
            tt("dve", t1, ai, lre, ALU.mult, R, R)
            tt("dve", t2, nr, lim, ALU.mult, R, R)
            tt("dve", t1, t1, t2, ALU.subtract, R, R)
            tt("dve", w0i, t1, den, ALU.mult, R, R)

            def cmul(o_r, o_i, xr, xi, yr, yi):
                tt("dve", t1, xr, yr, ALU.mult, R, R)
                tt("dve", t2, xi, yi, ALU.mult, R, R)
                tt("dve", o_r, t1, t2, ALU.subtract, R, R)
                tt("dve", t1, xr, yi, ALU.mult, R, R)
                tt("dve", t2, xi, yr, ALU.mult, R, R)
                tt("dve", o_i, t1, t2, ALU.add, R, R)
            cur = (w0r, w0i)
            nxt = (w1r, w1i)
            for k in range(nw):
                on_w(k, cur[0], cur[1], (x, th, t3))
                if k + 1 < nw:
                    cmul(nxt[0], nxt[1], cur[0], cur[1], ar, ai)
                    cur, nxt = nxt, cur
            if na >= 1:
                p0 = (c, s_)
                p1 = (w0r, w0i) if nw == 0 else cur
                p1 = nxt
                on_a(1, ar, ai, (x, th, t3))
                curp = (ar, ai)
                bufs = [p0, p1]
                for k in range(2, na + 1):
                    dst = bufs[k % 2]
                    cmul(dst[0], dst[1], curp[0], curp[1], ar, ai)
                    curp = dst
                    on_a(k, curp[0], curp[1], (x, th, t3))

        rin = A.f32(5 * 512)
        dma("sp", rin, s5rows_d, sem_c[2], [], ["rows"])
        rv = rin.rearrange("p (a f) -> p a f", a=5)
        eall = A.bf16(8 * 8 * 2 * 2 * 64).rearrange("p (f i r e s) -> p f i r e s", f=8, i=8, r=2, e=2)
        bre_r = rv[:, 3, :]
        bim_r = rv[:, 4, :]

        def on_w_rows(k, wr, wi, tmp):
            i = 7 - k
            t1r, t2r, t3r = tmp
            R = ["rows"]
            tt("dve", t1r, wr, bre_r, ALU.mult, R, R)
            tt("dve", t3r, wi, bim_r, ALU.mult, R, R)
            tt("dve", t1r, t1r, t3r, ALU.subtract, R, R)
            tt("dve", t2r, wr, bim_r, ALU.mult, R, R)
            tt("dve", t3r, wi, bre_r, ALU.mult, R, R)
            tt("dve", t2r, t2r, t3r, ALU.add, R, R)
            for ri, src in ((0, t1r), (1, t2r)):
                for e in range(2):
                    ts1("pool" if e else "dve", eall[:, :, i, ri, e, :], src.rearrange("p (f s) -> p f s", f=8),
                        V("mrow", e), ALU.mult, R + ["vec"], ["eall"])
        coef_chain(rv[:, 0, :], rv[:, 1, :], rv[:, 2, :], 512, A.f32, "rows", 8, 0, on_w_rows, None)
        eflat = eall.rearrange("p f i r e s -> p (f i r e s)")
        for t in range(4):
            dma("sp", wb[39 + t], eflat[:, t * 4096:(t + 1) * 4096], sem_s5[0], ["eall"], [("wb", 39 + t)], group=True)
        P.fence()
        A.reset(RES_END)

        pin = A.f32(96 + 2048)
        dma("sp", pin, s5pp_d, sem_c[3], [], ["pp"])
        small = A.f32(32 * 64)
        so = [0]

        def alloc_small(F):
            a = small[:, so[0]:so[0] + F]
            so[0] += F
            assert so[0] <= 32 * 64
            return a
        bre_p = pin[:, 96:96 + 512].rearrange("p (q h) -> p q h", q=32)
        bim_p = pin[:, 96 + 512:96 + 1024].rearrange("p (q h) -> p q h", q=32)
        cre_p = pin[:, 96 + 1024:96 + 1536].rearrange("p (q h) -> p q h", q=32)
        cim_p = pin[:, 96 + 1536:96 + 2048].rearrange("p (q h) -> p q h", q=32)
        u1 = A.f32(512).rearrange("p (q h) -> p q h", q=32)
        u2 = A.f32(512).rearrange("p (q h) -> p q h", q=32)
        u3 = A.f32(512).rearrange("p (q h) -> p q h", q=32)

        def bcq(a_):
            return a_.unsqueeze(2).to_broadcast([128, 32, 16])
        gall = A.bf16(32 * 8 * 2 * 2 * 16).rearrange("p (q j r e h) -> p q j r e h", q=32, j=8, r=2, e=2)
        lk = A.bf16(8 * 2 * 32 * 32).rearrange("p (k r q e h) -> p k r q e h", k=8, r=2, q=32, e=2)
        R = ["pp"]

        def on_w_pp(k, wr, wi, tmp):
            tt("dve", u1, bre_p, bcq(wr), ALU.mult, R, R)
            tt("dve", u3, bim_p, bcq(wi), ALU.mult, R, R)
            tt("dve", u1, u1, u3, ALU.subtract, R, R)
            tt("dve", u2, bim_p, bcq(wr), ALU.mult, R, R)
            tt("dve", u3, bre_p, bcq(wi), ALU.mult, R, R)
            tt("dve", u2, u2, u3, ALU.add, R, R)
            for ri, src in ((0, u1), (1, u2)):
                for e in range(2):
                    ts1("pool" if e else "dve", lk[:, k, ri, :, e, :], src, V("mpp", e), ALU.mult,
                        R + ["vec"], ["lk"])

        def on_a_pp(k, a_r, a_i, tmp):
            j = k - 1
            tt("dve", u1, cre_p, bcq(a_r), ALU.mult, R, R)
            tt("dve", u3, cim_p, bcq(a_i), ALU.mult, R, R)
            tt("dve", u1, u1, u3, ALU.subtract, R, R)
            tt("dve", u2, cre_p, bcq(a_i), ALU.mult, R, R)
            tt("dve", u3, cim_p, bcq(a_r), ALU.mult, R, R)
            tt("dve", u2, u2, u3, ALU.add, R, R)
            ts1("dve", u2, u2, -1.0, ALU.mult, R, R)
            for ri, src in ((0, u1), (1, u2)):
                for e in range(2):
                    ts1("pool" if e else "dve", gall[:, :, j, ri, e, :], src, V("mpp", e), ALU.mult,
                        R + ["vec"], ["gall"])
            if k == 8:
                for h2 in range(2):
                    cp("dve", ar8[:, h2 * 32:(h2 + 1) * 32], a_r, ["pp"], ["a8"])
                    cp("dve", ai8[:, h2 * 32:(h2 + 1) * 32], a_i, ["pp"], ["a8"])
        coef_chain(pin[:, 0:32], pin[:, 32:64], pin[:, 64:96], 32, alloc_small, "pp", 8, 8, on_w_pp, on_a_pp)
        gflat = gall.rearrange("p q j r e h -> p (q j r e h)")
        for t in range(4):
            dma("sp", wb[43 + t], gflat[:, t * 4096:(t + 1) * 4096], sem_s5[1], ["gall"], [("wb", 43 + t)], group=True)
        cpad = A.bf16(2 * 32 * 128).rearrange("p (r f l g h) -> p r f l g h", r=2, f=8, l=4, g=8)
        P.op("pool", lambda e: e.memset(cpad, 0.0), writes=["cpad"])
        ts1("dve", u2, cim_p, -1.0, ALU.mult, R, R)
        for ri, src in ((0, cre_p), (1, u2)):
            srcv = src.rearrange("p (f l) h -> p f l h", f=8)
            for ql in range(4):
                for e in range(2):
                    ts1("dve", cpad[:, ri, :, ql, 2 * ql + e, :], srcv[:, :, ql, :], V("mpp", e), ALU.mult,
                        R + ["vec", "cpad"], ["cpad"])
        ddiag = A.f32(128 * 8).rearrange("p (f m) -> p f m", f=8)
        for fc in range(8):
            ts1("dve", ddiag[:, fc, :], cst[:, 0:128], V("d", fc), ALU.mult, ["cst", "vec"], ["ddiag"])
        kb = [0]
        for fc in range(8):
            for k in range(8):
                bank = kb[0] % 7
                kb[0] += 1
                for ql in range(4):
                    q = 4 * fc + ql
                    for ri in range(2):
                        pe_mm(ps[32 * ql:32 * ql + 32, bank, 0:128],
                              lk[:, k, ri, q, :, :].rearrange("p e h -> p (e h)"),
                              cpad[:, ri, fc, ql, :, :].rearrange("p g h -> p (g h)"),
                              ri == 0, ri == 1, ["lk", "cpad"], [("ps", bank)], tp=(0, 32 * ql))
                if k == 0:
                    tt("dve", kall[:, fc, k, :], ps[:, bank, 0:128], ddiag[:, fc, :], ALU.add,
                       [("ps", bank), "ddiag"], ["kall"])
                else:
                    cp("act" if (k % 2) else "dve", kall[:, fc, k, :], ps[:, bank, 0:128], [("ps", bank)], ["kall"])
        if "kall" in dbg_out:
            kf = A.f32(8 * 8 * 128)
            cp("dve", kf, kall.rearrange("p f k m -> p (f k m)"), ["kall"], ["kf"])
            dbg_dump("kall", kf, ["kf"])

        P.fence()

        A.reset(RES_END)
        HB_ = [A.f32(8 * 512).rearrange("p (f t) -> p f t", f=8) for _ in range(2)]
        X = [A.bf16(8 * 512).rearrange("p (f t) -> p f t", f=8) for _ in range(7)]
        sall = A.f32(65 * 64).rearrange("p (c s) -> p c s", c=65)
        sbf = A.bf16(64 * 64).rearrange("p (s c) -> p s c", s=64)
        NFS = 6
        FS = [A.f32(516) for _ in range(NFS)]
        WBUF = [A.bf16(4096) for _ in range(NWB)]
        st = A.f32(8 * 128).rearrange("p (h v) -> p h v", h=8)
        stm = A.f32(8 * 128).rearrange("p (h v) -> p h v", h=8)
        stb = A.bf16(2 * 4 * 128).rearrange("p (r b v) -> p r b v", r=2, b=4)
        scm = A.bf16(2 * 512).rearrange("p (r t) -> p r t", r=2)
        osq = A.bf16(512)
        sog = A.bf16(512)
        kot = A.bf16(512)
        gsg = A.bf16(4 * 512).rearrange("p (m t) -> p m t", m=4)
        dec = A.f32(32).rearrange("p (h b) -> p h b", h=8)
        smeta = A.f32(64)
        sct = A.f32(128).rearrange("p (a s) -> p a s", a=2)
        uph = A.f32(88).rearrange("p (c j) -> p c j", c=44)
        uphm = A.f32(88).rearrange("p (c j) -> p c j", c=44)
        print("arena used words", A.off, "of", ARENA_WORDS)

        zb, X1, X2, X3, X4, X5, X6 = X
        vt = X4.rearrange("p f t -> p (f t)").rearrange("p (b v) -> p b v", b=4)
        kout = X5.rearrange("p f t -> p (f t)").rearrange("p (b h d) -> p b h d", b=4, h=8)

        P.op("dve", lambda e: e.memset(sall[:, 0, :], 0.0), writes=[("sall", 0)])
        P.op("dve", lambda e: e.memset(st, 0.0), writes=[("st", h) for h in range(8)])
        P.op("dve", lambda e: e.memset(uph, 0.0), writes=["uph"])

        pb = [0]

        def pbank():
            b = pb[0] % 7
            pb[0] += 1
            return b
        fsn = [0]

        def fs():
            i = fsn[0] % NFS
            fsn[0] += 1
            return FS[i], ("fs", i)

        per_tile = ([0, 1, 39, 40, 41, 42, 2, 3, 4, 5, 6, 7, 8, 9, 43, 44, 45, 46, 14, 15,
                     10, 16, 11, 17, 12, 18, 13, 19, 20, 21]
                    + [22, 23, 24, 25, 33, 34, 26, 27, 28, 29, 35, 36, 30, 31, 32, 37, 38])
        n_tok_tiles = 1 + n_seq * n_tiles_per_seq
        wseq = per_tile * n_tok_tiles
        wstate = {"issued": 0, "next": 0}

        def w_issue(upto):
            while wstate["issued"] < min(upto, len(wseq)):
                n = wstate["issued"]
                slot = n % NWB
                tid = wseq[n]
                dma("sp", WBUF[slot], wb[tid], sem_w[slot], [("wb", tid)], [("wbuf", slot)])
                wstate["issued"] += 1

        def wget(expect):
            n = wstate["next"]
            assert wseq[n] == expect, (n, wseq[n], expect)
            w_issue(n + NWB)
            wstate["next"] += 1
            slot = n % NWB
            return WBUF[slot], ("wbuf", slot)

        def load_h(tile, Hbuf, hkey, par):
            kind, b, t0, Tn = tile
            if kind == "meta":
                dma("sp", Hbuf[:, :, 0:Tn], metaT, sem_x[par], [], [(hkey, fc) for fc in range(8)])
            else:
                dma("sp", Hbuf[:, :, 0:Tn], xT[b, :, :, t0:t0 + Tn], sem_x[par], [], [(hkey, fc) for fc in range(8)])

        def rmsnorm(Hbuf, hkey, gname, Tn, out_buf, okey, sqbuf, sqkey, inplace=False):
            for fc in range(8):
                act(sqbuf[:, fc, :Tn], Hbuf[:, fc, :Tn], AF.Square, [(hkey, fc)], [(sqkey, fc)])
            bank = pbank()
            for fc in range(8):
                pe_mm(ps[:, bank, :Tn], ones_bf, sqbuf[:, fc, :Tn], fc == 0, fc == 7,
                      ["ones", (sqkey, fc)], [("ps", bank)])
            rs, rk = fs()
            act(rs[:, :Tn], ps[:, bank, :Tn], AF.Sqrt, [("ps", bank), "vec"], [rk], bias=V("eps"), scale=1.0 / D)
            P.op("dve", lambda e: e.reciprocal(rs[:, :Tn], rs[:, :Tn]), reads=[rk], writes=[rk])
            for fc in range(8):
                stt("dve", out_buf[:, fc, :Tn], Hbuf[:, fc, :Tn], V(gname, fc), rs[:, :Tn], ALU.mult, ALU.mult,
                    [(hkey, fc), rk, "vec"], [(okey, fc)])

        def proj_fm(wt, wkey, rhs_buf, rkey, Tn, evac, nk=8):
            for mi in range(4):
                bank = pbank()
                for kc in range(nk):
                    pe_mm(ps[:, bank, :Tn], wt[:, kc * 512 + mi * 128: kc * 512 + mi * 128 + 128],
                          rhs_buf[:, kc, :Tn], kc == 0, kc == nk - 1, [wkey, (rkey, kc)], [("ps", bank)])
                evac(mi, bank)

        tile_ctr = [0]

        def stage(name):
            if stop_after is None:
                return
            if stop_after == name or (tile_ctr[0] >= 2 and stop_after == "main:" + name):
                raise StopBuild()

        def do_tile(tile, Hbuf, hkey, Ebuf, ekey, next_load):
            kind, b, t0, Tn = tile
            is_meta = kind == "meta"
            NCH = Tn // 8
            bs = min(128, Tn)
            nblk = Tn // bs
            tix = tile_ctr[0]
            tile_ctr[0] += 1
            want_dbg = dbg is not None and tix == dbg.get("_tile", -1)

            rmsnorm(Hbuf, hkey, "g1", Tn, zb, "z", X1, "x1")
            stage("norm1")
            for t in range(2):
                wt, wk = wget(t)

                def ev_u(mi, bank, t=t):
                    m = 4 * t + mi
                    cp("act" if mi % 2 == 0 else "dve", X1[:, m, :Tn], ps[:, bank, :Tn], [("ps", bank)], [("x1", m)])
                proj_fm(wt, wk, zb, "z", Tn, ev_u)
            stage("uproj")
            for t in range(4):
                wt, wk = wget(39 + t)
                wE = wt.rearrange("p (f i r m) -> p f i r m", f=2, i=8, r=2)
                for fcl in range(2):
                    fc = 2 * t + fcl
                    banks = [pbank() for _ in range(4)]
                    uv = X1[:, fc, :Tn].rearrange("p (c j) -> p c j", j=8)
                    for ri in range(2):
                        for i in range(8):
                            for ql in range(4):
                                pe_mm(ps[:, banks[ql], ri * NCH:(ri + 1) * NCH], wE[32 * ql:32 * ql + 32, fcl, i, ri, :],
                                      uv[32 * ql:32 * ql + 32, :, i], i == 0, i == 7,
                                      [wk, ("x1", fc)], [("ps", banks[ql])], tp=(32 * ql, 0))
                    for ql in range(4):
                        q = 4 * fc + ql
                        dst = sall[:, 1:1 + NCH, :].rearrange("p c (r q) -> p q r c", r=2)[:, q, :, :]
                        src = ps[:, banks[ql], 0:2 * NCH].rearrange("p (r c) -> p r c", r=2)
                        cp("act" if ql % 2 == 0 else "dve", dst, src, [("ps", banks[ql])],
                           [("sall", 1 + c) for c in range(NCH)])
            stage("eproj")
            SE = SCAN_ENG
            scan_pos = [1]

            def scan_some(nsteps):
                for _ in range(nsteps):
                    c = scan_pos[0]
                    if c > NCH:
                        return
                    scan_pos[0] += 1
                    prev = sall[:, c - 1, :]
                    cur = sall[:, c, :]
                    kp = ("sall", c - 1)
                    kc_ = ("sall", c)
                    tt(SE, sct[:, 0, :], prev, ar8, ALU.mult, [kp, "a8"], ["sct0"])
                    tt(SE, sct[:, 1, :], prev, ai8, ALU.mult, [kp, "a8"], ["sct1"])
                    tt(SE, sct[:, 0, :], sct[:, 0, :], cur, ALU.add, ["sct0", kc_], ["sct0"])
                    tt(SE, cur[:, 0:32], sct[:, 0, 0:32], sct[:, 1, 32:64], ALU.subtract, ["sct0", "sct1"], [kc_])
                    tt(SE, cur[:, 32:64], sct[:, 0, 32:64], sct[:, 1, 0:32], ALU.add, ["sct0", "sct1", kc_], [kc_])

            def scan_finish():
                scan_some(NCH)
                cp("dve", sbf[:, :, 0:NCH], sall[:, 0:NCH, :].rearrange("p c s -> p s c"),
                   [("sall", c) for c in range(NCH)], ["sbf"])
                if is_meta:
                    cp("dve", smeta, sall[:, NCH, :], [("sall", NCH)], ["smeta"])
                else:
                    cp("dve", sall[:, 0, :], sall[:, NCH, :], [("sall", NCH), "sbf"], [("sall", 0)])
            stage("scan")
            for t in range(2):
                wt, wk = wget(2 + t)

                def ev_f(mi, bank, t=t):
                    hd = 4 * t + mi
                    g, gk = fs()
                    lf, lk_ = fs()
                    kk, kkk = fs()
                    cm, cmk = fs()
                    act(g[:, :Tn], ps[:, bank, :Tn], AF.Sigmoid, [("ps", bank)], [gk])
                    act(lf[:, :Tn], g[:, :Tn], AF.Ln, [gk, "lbv"], [lk_], bias=lbv[:, 0, hd:hd + 1], scale=lbv[:, 1, hd:hd + 1])
                    ts("dve", kk[:, :Tn], g[:, :Tn], lbv[:, 2, hd:hd + 1], lbv[:, 1, hd:hd + 1], ALU.mult, ALU.add,
                       [gk, "lbv"], [kkk])
                    P.op("dve", lambda e: e.tensor_tensor_scan(cm[:, :Tn], scanmsk[:, :Tn], lf[:, :Tn], 0.0, ALU.mult, ALU.add),
                         reads=["scanmsk", lk_], writes=[cmk])
                    act(Ebuf[:, hd, :Tn], cm[:, :Tn], AF.Exp, [cmk], [(ekey, hd)])
                    act(lf[:, :Tn], cm[:, :Tn], AF.Exp, [cmk, lk_], [lk_], scale=-1.0)
                    tt("pool", kk[:, :Tn], kk[:, :Tn], lf[:, :Tn], ALU.mult, [kkk, lk_], [kkk])
                    cp("pool", X2[:, hd, :Tn], kk[:, :Tn], [kkk], [("x2", hd)])
                    cp("dve", dec[:, hd, 0:nblk], Ebuf[:, hd, :Tn].rearrange("p (b j) -> p b j", j=bs)[:, :, bs - 1],
                       [(ekey, hd)], [("dec", hd)])
                    tt("pool", kot[:, :Tn].rearrange("p (b j) -> p b j", j=bs), kk[:, :Tn].rearrange("p (b j) -> p b j", j=bs),
                       dec[:, hd, 0:nblk].unsqueeze(2).to_broadcast([128, nblk, bs]), ALU.mult,
                       [kkk, ("dec", hd)], ["kot"])
                    half = hd % 2
                    for blk in range(nblk):
                        P.op("pe", lambda e, blk=blk: e.transpose(pst[0:bs, half * 512 + blk * 128: half * 512 + blk * 128 + 128],
                                                                  kot[:, blk * bs:(blk + 1) * bs], ident_bf),
                             reads=["kot", "ident"], writes=[("pst", half)])
                    cp("act", kout[0:bs, 0:nblk, hd, :],
                       pst[0:bs, half * 512: half * 512 + nblk * 128].rearrange("p (b d) -> p b d", b=nblk),
                       [("pst", half)], [("x5", hd)])
                    scan_some(8)
                proj_fm(wt, wk, zb, "z", Tn, ev_f)
            scan_finish()
            stage("fproj")
            for t in range(2):
                wt, wk = wget(4 + t)

                def ev_q(mi, bank, t=t):
                    hd = 4 * t + mi
                    tt("dve", X3[:, hd, :Tn], ps[:, bank, :Tn], Ebuf[:, hd, :Tn], ALU.mult,
                       [("ps", bank), (ekey, hd)], [("x3", hd)])
                proj_fm(wt, wk, zb, "z", Tn, ev_q)
            stage("qproj")
            for t in range(2):
                wt, wk = wget(6 + t)
                for blk in range(nblk):
                    bank = pbank()
                    for kc in range(8):
                        pe_mm(ps[0:bs, bank, :], zb[:, kc, blk * bs:(blk + 1) * bs], wt[:, kc * 512:(kc + 1) * 512],
                              kc == 0, kc == 7, [wk, ("z", kc)], [("ps", bank)])
                    cp("act" if blk % 2 == 0 else "dve", vt[0:bs, blk, t * 512:(t + 1) * 512], ps[0:bs, bank, :],
                       [("ps", bank)], [("x4", 2 * blk + t)])
            stage("iproj")
            for hd in range(8):
                if hd % 4 == 0:
                    wog, wogk = wget(8 + hd // 4)
                rot = hd % 2
                bS = pbank()
                for blk in range(nblk):
                    tok = slice(blk * bs, (blk + 1) * bs)
                    pe_mm(ps[0:bs, bS, blk * 128: blk * 128 + bs], X2[:, hd, tok], X3[:, hd, tok], True, True,
                          [("x2", hd), ("x3", hd)], [("ps", bS)])
                tt("dve", scm[0:bs, rot, 0:nblk * bs].rearrange("p (b t) -> p b t", b=nblk),
                   ps[0:bs, bS, 0:nblk * 128].rearrange("p (b t) -> p b t", b=nblk)[:, :, 0:bs],
                   maskT[0:bs, 0:bs].unsqueeze(1).to_broadcast([bs, nblk, bs]), ALU.mult,
                   [("ps", bS), "cst"], [("scm", rot)])
                bU = pbank()
                for blk in range(nblk):
                    pe_mm(ps[:, bU, blk * 128:(blk + 1) * 128], kout[0:bs, blk, hd, :], vt[0:bs, blk, hd * 128:(hd + 1) * 128],
                          True, True, [("x5", hd), ("x4", 2 * blk + hd // 4)], [("ps", bU)])
                for blk in range(nblk):
                    cp("act", stb[:, rot, blk, :], st[:, hd, :], [("st", hd)], [("stb", rot, blk)])
                    stt("dve", st[:, hd, :], st[:, hd, :], dec[:, hd, blk:blk + 1], ps[:, bU, blk * 128:(blk + 1) * 128],
                        ALU.mult, ALU.add, [("st", hd), ("dec", hd), ("ps", bU)], [("st", hd)])
                bO = pbank()
                for blk in range(nblk):
                    tok = slice(blk * bs, (blk + 1) * bs)
                    pe_mm(ps[:, bO, blk * bs:(blk + 1) * bs], vt[0:bs, blk, hd * 128:(hd + 1) * 128],
                          scm[0:bs, rot, blk * bs:(blk + 1) * bs], True, False,
                          [("x4", 2 * blk + hd // 4), ("scm", rot)], [("ps", bO)])
                    pe_mm(ps[:, bO, blk * bs:(blk + 1) * bs], stb[:, rot, blk, :], X3[:, hd, tok], False, True,
                          [("stb", rot, blk), ("x3", hd)], [("ps", bO)])
                osb, osk = fs()
                cp("dve", osb[:, :Tn], ps[:, bO, :Tn], [("ps", bO)], [osk])
                act(osq[:, :Tn], osb[:, :Tn], AF.Square, [osk], ["osq"])
                bN = pbank()
                pe_mm(ps[:, bN, :Tn], ones_bf, osq[:, :Tn], True, True, ["ones", "osq"], [("ps", bN)])
                rs, rk = fs()
                act(rs[:, :Tn], ps[:, bN, :Tn], AF.Sqrt, [("ps", bN), "vec"], [rk], bias=V("eps"), scale=1.0 / 128)
                P.op("dve", lambda e, rs=rs: e.reciprocal(rs[:, :Tn], rs[:, :Tn]), reads=[rk], writes=[rk])
                bG = pbank()
                mi = hd % 4
                for kc in range(8):
                    pe_mm(ps[:, bG, :Tn], wog[:, kc * 512 + mi * 128: kc * 512 + mi * 128 + 128], zb[:, kc, :Tn],
                          kc == 0, kc == 7, [wogk, ("z", kc)], [("ps", bG)])
                act(sog[:, :Tn], ps[:, bG, :Tn], AF.Silu, [("ps", bG)], ["sog"])
                stt("dve", osb[:, :Tn], osb[:, :Tn], V("ng"), rs[:, :Tn], ALU.mult, ALU.mult, [osk, rk, "vec"], [osk])
                tt("pool", X6[:, hd, :Tn], osb[:, :Tn], sog[:, :Tn], ALU.mult, [osk, "sog"], [("x6", hd)])
            if is_meta:
                cp("dve", stm, st, [("st", h) for h in range(8)], ["stm"])
            if next_load is not None:
                next_load()
            if want_dbg:
                dbg_dump("yb", X6.rearrange("p f t -> p (f t)"), [("x6", h) for h in range(8)])
                dbg_dump("uT", X1.rearrange("p f t -> p (f t)"), [("x1", h) for h in range(8)])

            stage("hgrn")
            for fc in range(8):
                if fc % 2 == 0:
                    wG_, wGk = wget(43 + fc // 2)
                    wG = wG_.rearrange("p (q j r m) -> p q j r m", q=8, j=8, r=2)
                bY = pbank()
                yv = ps[:, bY, 0:Tn].rearrange("p (c j) -> p c j", j=8)
                uv = X1[:, fc, :Tn].rearrange("p (c j) -> p c j", j=8)
                for k in range(8):
                    pe_mm(yv[:, :, k:8], kall[:, fc, k, :], uv[:, :, 0:8 - k], k == 0, False,
                          ["kall", ("x1", fc)], [("ps", bY)])
                for ql in range(4):
                    q = 4 * fc + ql
                    q8 = q % 8
                    for j in range(8):
                        for ri in range(2):
                            last = (j == 7 and ri == 1)
                            pe_mm(yv[32 * ql:32 * ql + 32, :, j], wG[:, q8, j, ri, :], sbf[:, ri * 32 + q, 0:NCH],
                                  False, last, [wGk, "sbf"], [("ps", bY)], tp=(0, 32 * ql))
                if USE_GELU_LUT:
                    act(X4[:, fc, :Tn], ps[:, bY, :Tn], AF.Gelu_apprx_tanh, [("ps", bY)], [("x4", fc)])
                else:
                    xs, xk = fs()
                    x2, x2k = fs()
                    cp("act", xs[:, :Tn], ps[:, bY, :Tn], [("ps", bY)], [xk])
                    act(x2[:, :Tn], ps[:, bY, :Tn], AF.Square, [("ps", bY)], [x2k])
                    ts("dve", x2[:, :Tn], x2[:, :Tn], 0.044715, 1.0, ALU.mult, ALU.add, [x2k], [x2k])
                    tt("pool", x2[:, :Tn], x2[:, :Tn], xs[:, :Tn], ALU.mult, [x2k, xk], [x2k])
                    act(x2[:, :Tn], x2[:, :Tn], AF.Sigmoid, [x2k], [x2k], scale=1.5957691216057308)
                    tt("pool", X4[:, fc, :Tn], xs[:, :Tn], x2[:, :Tn], ALU.mult, [xk, x2k], [("x4", fc)])
            if want_dbg:
                dbg_dump("ya", X4.rearrange("p f t -> p (f t)"), [("x4", h) for h in range(8)])
            stage("s5out")
            for t in range(2):
                wt, wk = wget(14 + t)

                def ev_glu(mi, bank, t=t):
                    m = 4 * t + mi
                    sg, sgk = fs()
                    act(sg[:, :Tn], ps[:, bank, :Tn], AF.Sigmoid, [("ps", bank)], [sgk])
                    tt("dve" if mi % 2 == 0 else "pool", X1[:, m, :Tn], X4[:, m, :Tn], sg[:, :Tn], ALU.mult,
                       [("x4", m), sgk], [("x1", m)])
                proj_fm(wt, wk, X4, "x4", Tn, ev_glu)
            stage("glu")
            for t in range(2):
                wt, wk = wget(10 + t)

                def ev_ga(mi, bank):
                    act(gsg[:, mi, :Tn], ps[:, bank, :Tn], AF.Sigmoid, [("ps", bank)], [("gsg", mi)])
                proj_fm(wt, wk, zb, "z", Tn, ev_ga)
                wt, wk = wget(16 + t)

                def ev_sp(mi, bank, t=t):
                    m = 4 * t + mi
                    tt("dve", X2[:, m, :Tn], ps[:, bank, :Tn], gsg[:, mi, :Tn], ALU.mult,
                       [("ps", bank), ("gsg", mi)], [("x2", m)])
                proj_fm(wt, wk, X1, "x1", Tn, ev_sp)
            for t in range(2):
                wt, wk = wget(12 + t)
                proj_fm(wt, wk, zb, "z", Tn, ev_ga)
                wt, wk = wget(18 + t)

                def ev_hp(mi, bank, t=t):
                    m = 4 * t + mi
                    tmp, tk = fs()
                    tt("dve", tmp[:, :Tn], ps[:, bank, :Tn], gsg[:, mi, :Tn], ALU.mult,
                       [("ps", bank), ("gsg", mi)], [tk])
                    tt("pool", X3[:, m, :Tn], tmp[:, :Tn], X2[:, m, :Tn], ALU.add, [tk, ("x2", m)], [("x3", m)])
                proj_fm(wt, wk, X6, "x6", Tn, ev_hp)
            stage("merge")
            for t in range(2):
                wt, wk = wget(20 + t)

                def ev_o(mi, bank, t=t):
                    m = 4 * t + mi
                    tt("dve", Hbuf[:, m, :Tn], ps[:, bank, :Tn], Hbuf[:, m, :Tn], ALU.add,
                       [("ps", bank), (hkey, m)], [(hkey, m)])
                proj_fm(wt, wk, X3, "x3", Tn, ev_o)
            if want_dbg:
                dbg_dump("h1", Hbuf.rearrange("p f t -> p (f t)"), [(hkey, h) for h in range(8)])
            stage("wout")
            rmsnorm(Hbuf, hkey, "g2", Tn, zb, "z", X1, "x1")
            stage("norm2")
            for r in range(3):
                tiles_r = list(range(4 * r, min(4 * r + 4, 11)))
                for i in tiles_r:
                    wt, wk = wget(22 + i)
                    banks = []
                    cvals = []

                    def ev_up(mi, bank, i=i, cvals=cvals):
                        h2 = mi // 2
                        j = 2 * i + h2
                        ci = j if mi % 2 == 0 else 22 + j
                        cin, ck = fs()
                        cc, cck = fs()
                        cp("act", cin[:, 2:2 + Tn], ps[:, bank, :Tn], [("ps", bank)], [ck])
                        ts("dve", cc[:, :Tn], cin[:, 2:2 + Tn], V("cw2", ci), V("cb", ci), ALU.mult, ALU.add,
                           [ck, "vec"], [cck])
                        cp("pool", cin[:, 0:2], uph[:, ci, :], [("uph", ci), ck], [ck])
                        cp("pool", uph[:, ci, :], cin[:, Tn:Tn + 2], [ck], [("uph", ci)])
                        stt("dve", cc[:, :Tn], cin[:, 1:1 + Tn], V("cw1", ci), cc[:, :Tn], ALU.mult, ALU.add,
                            [ck, cck, "vec"], [cck])
                        ts1("pool", cin[:, 0:Tn], cin[:, 0:Tn], V("cw0", ci), ALU.mult, [ck, "vec"], [ck])
                        tt("pool", cc[:, :Tn], cc[:, :Tn], cin[:, 0:Tn], ALU.add, [ck, cck], [cck])
                        cvals.append((cc, cck))
                        if mi % 2 == 1:
                            ca, cak = cvals[-2]
                            cb_, cbk = cvals[-1]
                            jl = j - 8 * r
                            act(ca[:, :Tn], ca[:, :Tn], AF.Silu, [cak], [cak])
                            tt("dve", X1[:, jl, :Tn], ca[:, :Tn], cb_[:, :Tn], ALU.mult, [cak, cbk], [("x1", jl)])
                    proj_fm(wt, wk, zb, "z", Tn, ev_up)
                nk = 8 if r < 2 else 6
                for ch in range(2):
                    wt, wk = wget(33 + 2 * r + ch)

                    def ev_dn(mi, bank, ch=ch):
                        m = 4 * ch + mi
                        tt("dve", Hbuf[:, m, :Tn], ps[:, bank, :Tn], Hbuf[:, m, :Tn], ALU.add,
                           [("ps", bank), (hkey, m)], [(hkey, m)])
                    proj_fm(wt, wk, X1, "x1", Tn, ev_dn, nk=nk)
            if is_meta:
                cp("dve", uphm, uph, [("uph", c) for c in range(44)], ["uphm"])
                stage("meta_done")
                return
            rmsnorm(Hbuf, hkey, "g3", Tn, Hbuf, hkey, X1, "x1")
            dma("sp", outT[b, :, :, t0:t0 + Tn], Hbuf[:, :, 0:Tn], sem_o[0 if hkey == "hA" else 1], [(hkey, fc) for fc in range(8)],
                [("out", b, t0)])

        tiles = [("meta", 0, 0, NMETA)] if stop_after != "prologue" else []
        for b in range(n_seq if stop_after != "prologue" else 0):
            for ti in range(n_tiles_per_seq):
                tiles.append(("main", b, ti * TT, TT))
        hkeys = ["hA", "hB"]
        if stop_after == "prologue":
            n_seq = 0
        if tiles:
            load_h(tiles[0], HB_[0], hkeys[0], 0)
        for n, tile in enumerate(tiles):
            par = n % 2
            Hbuf, hkey = HB_[par], hkeys[par]
            Ebuf, ekey = HB_[1 - par], hkeys[1 - par]
            if tile[0] == "main" and tile[2] == 0:
                cp("dve", sall[:, 0, :], smeta, ["smeta"], [("sall", 0)])
                cp("dve", st, stm, ["stm"], [("st", h) for h in range(8)])
                cp("dve", uph, uphm, ["uphm"], [("uph", c) for c in range(44)])
            nl = None
            if n + 1 < len(tiles):
                nl = (lambda n=n, par=par: load_h(tiles[n + 1], HB_[1 - par], hkeys[1 - par], 1 - par))
            try:
                do_tile(tile, Hbuf, hkey, Ebuf, ekey, nl)
            except StopBuild:
                break
        P.op("sp", lambda e: e.nop(), extra_deps=list(P.last_dma.values()))
        P.finalize(sems)
        print("instr counts", {e: len(P.streams[e]) for e in ENGS})

        @block.tensor
        def _(e):
            P.emit("pe", e)

        @block.vector
        def _(e):
            P.emit("dve", e)

        @block.scalar
        def _(e):
            P.emit("act", e)

        @block.gpsimd
        def _(e):
            P.emit("pool", e)

        @block.sync
        def _(e):
            P.emit("sp", e)
    return nc


def kernel(**inputs):
    x = np.asarray(inputs["x"], np.float32)
    packed = pack_inputs(inputs)
    nc = build_program()
    in_maps = []
    for c in range(NCORES):
        xs = x[c * SPC:(c + 1) * SPC]
        xT = np.ascontiguousarray(xs.transpose(0, 2, 1).reshape(SPC, 8, 128, SEQ).transpose(0, 2, 1, 3))
        m = dict(packed)
        m["xT"] = xT
        in_maps.append(m)
    res = run_bass_kernel_spmd(nc, in_maps, core_ids=list(range(NCORES)))
    out = np.empty((NCORES * SPC, SEQ, D), np.float32)
    for c in range(NCORES):
        oT = np.asarray(res.results[c]["outT"]).reshape(SPC, 128, 8, SEQ)
        out[c * SPC:(c + 1) * SPC] = oT.transpose(0, 3, 2, 1).reshape(SPC, SEQ, D)
    return out
```

```python
import math
import numpy as np
import concourse.bass as bass
import concourse.mybir as mybir
from concourse.bass_utils import run_bass_kernel_spmd

F32 = mybir.dt.float32
BF16 = mybir.dt.bfloat16
ALU = mybir.AluOpType
AF = mybir.ActivationFunctionType

NCORES = 8
SPC = 4
D = 1024
SEQ = 2048
NMETA = 16
TT = 512
DFF = 2816
EPS = 1e-6
NWB = 5
NWT = 47
SCAN_ENG = "dve"
USE_GELU_LUT = False

ENGS = ("pe", "dve", "act", "pool", "sp")


class StopBuild(Exception):
    pass


class Instr:
    __slots__ = ("eng", "fn", "deps", "pos", "marked", "sem", "val", "dma")

    def __init__(self, eng, fn):
        self.eng = eng
        self.fn = fn
        self.deps = set()
        self.marked = False
        self.sem = None
        self.val = 0
        self.dma = False
        self.pos = 0


class Prog:
    WIN = 3

    def __init__(self):
        self.streams = {e: [] for e in ENGS}
        self.res = {}
        self.dma_counts = {}
        self.last_dma = {}
        self.group_sems = set()

    def op(self, eng, fn, reads=(), writes=(), dma_sem=None, extra_deps=(), group=False):
        ins = Instr(eng, fn)
        deps = set(extra_deps)
        if eng != "pe":
            psr = [r for r in reads if isinstance(r, tuple) and r[0] in ("ps", "pst")]
            if psr:
                reads = [r for r in reads if r not in psr]
                writes = list(writes) + psr
        for r in reads:
            st = self.res.get(r)
            if st is not None and st[0] is not None:
                deps.add(st[0])
        for r in writes:
            st = self.res.get(r)
            if st is not None:
                if st[0] is not None:
                    deps.add(st[0])
                deps.update(st[1])
        for r in reads:
            st = self.res.get(r)
            if st is None:
                self.res[r] = (None, [ins])
            else:
                st[1].append(ins)
        for r in writes:
            self.res[r] = (ins, [])
        deps.discard(ins)
        ins.deps = deps
        ins.pos = len(self.streams[eng])
        if dma_sem is not None:
            ins.dma = True
            ins.marked = True
            ins.sem = dma_sem
            c = self.dma_counts.get(id(dma_sem), 0) + 16
            self.dma_counts[id(dma_sem)] = c
            ins.val = c
            self.last_dma[id(dma_sem)] = ins
            if group:
                self.group_sems.add(id(dma_sem))
        self.streams[eng].append(ins)
        return ins

    def fence(self):
        lasts = [self.streams[e][-1] for e in ENGS if self.streams[e]]
        lasts += list(self.last_dma.values())
        for e in ENGS:
            self.op(e, lambda eng: eng.nop(), extra_deps=lasts)
        self.res = {}

    def _skip(self, ins, d):
        if d.dma or ins.dma:
            return False
        if d.eng != ins.eng:
            return False
        if d.eng == "pe":
            return True
        return ins.pos - d.pos > self.WIN

    def finalize(self, eng_sems):
        for e in ENGS:
            for ins in self.streams[e]:
                if ins.dma and id(ins.sem) in self.group_sems:
                    ins.val = self.dma_counts[id(ins.sem)]
        for e in ENGS:
            for ins in self.streams[e]:
                for d in ins.deps:
                    if d.dma or self._skip(ins, d):
                        continue
                    d.marked = True
        for e in ENGS:
            c = 0
            for ins in self.streams[e]:
                if ins.dma:
                    continue
                ins.sem = eng_sems[e]
                if ins.marked:
                    c += 1
                    ins.val = c

    def emit(self, eng_name, eng):
        waited = {}
        for ins in self.streams[eng_name]:
            need = {}
            for d in ins.deps:
                if not d.marked or self._skip(ins, d):
                    continue
                k = id(d.sem)
                if k not in need or need[k][1] < d.val:
                    need[k] = (d.sem, d.val)
            for k, (sem, val) in need.items():
                if waited.get(k, 0) < val:
                    eng.wait_ge(sem, val)
                    waited[k] = val
            bi = ins.fn(eng)
            if ins.marked:
                bi.then_inc(ins.sem, 16 if ins.dma else 1)


VEC = {}
_o = 0
for _n, _w in (("g1", 8), ("g2", 8), ("g3", 8), ("lbl0", 8), ("lbl1", 8), ("ng", 1), ("d", 8),
               ("cw0", 44), ("cw1", 44), ("cw2", 44), ("cb", 44), ("mrow", 2), ("mpp", 2),
               ("eps", 1), ("hpi", 1), ("one", 1)):
    VEC[_n] = _o
    _o += _w
NV = _o


def _wtile(wsub):
    nk = wsub.shape[0] // 128
    t = np.zeros((128, 8, 512), np.float32)
    t[:, :nk, :] = wsub.reshape(nk, 128, 512).transpose(1, 0, 2)
    return t.reshape(128, 4096)


def _chunked(v):
    return np.ascontiguousarray(v.reshape(-1, 128).T)


def pack_inputs(inp):
    f = np.float32
    w_in = np.asarray(inp["w_in"], f)[0]
    tiles = []
    offs = {"u": 0, "q": 1024, "f": 2048, "i": 3072, "og": 4096, "ga": 5120, "gb": 6144}
    for s in ("u", "f", "q", "i", "og", "ga", "gb"):
        for h in range(2):
            tiles.append(_wtile(w_in[:, offs[s] + h * 512: offs[s] + (h + 1) * 512]))
    for name in ("ssm_w_glu", "w_ssm_proj", "w_hgrn_proj", "w_out"):
        w = np.asarray(inp[name], f)[0]
        for h in range(2):
            tiles.append(_wtile(w[:, h * 512:(h + 1) * 512]))
    w_up = np.asarray(inp["w_up"], f)[0]
    for i in range(11):
        cols = np.concatenate([np.arange(2 * i * 128, 2 * i * 128 + 128), DFF + np.arange(2 * i * 128, 2 * i * 128 + 128),
                               np.arange((2 * i + 1) * 128, (2 * i + 1) * 128 + 128),
                               DFF + np.arange((2 * i + 1) * 128, (2 * i + 1) * 128 + 128)])
        tiles.append(_wtile(w_up[:, cols]))
    w_down = np.asarray(inp["w_down"], f)[0]
    for r in range(3):
        rows = slice(r * 1024, min((r + 1) * 1024, DFF))
        for ch in range(2):
            tiles.append(_wtile(w_down[rows, ch * 512:(ch + 1) * 512]))
    wf = np.stack(tiles, 0)
    assert wf.shape[0] == 39

    vec = np.zeros((128, NV), f)
    vec[:, VEC["g1"]:VEC["g1"] + 8] = _chunked(np.asarray(inp["mix_norm_g"], f)[0])
    vec[:, VEC["g2"]:VEC["g2"] + 8] = _chunked(np.asarray(inp["ffn_norm_g"], f)[0])
    vec[:, VEC["g3"]:VEC["g3"] + 8] = _chunked(np.asarray(inp["final_norm_g"], f))
    vec[:, VEC["lbl0"]:VEC["lbl0"] + 8] = _chunked(np.asarray(inp["hgrn_lb_logits"], f)[0])
    vec[:, VEC["lbl1"]:VEC["lbl1"] + 8] = _chunked(np.asarray(inp["hgrn_lb_logits"], f)[1])
    vec[:, VEC["ng"]] = np.asarray(inp["hgrn_norm_g"], f)[0]
    vec[:, VEC["d"]:VEC["d"] + 8] = _chunked(np.asarray(inp["ssm_d"], f)[0])
    cw = np.asarray(inp["conv_w"], f)[0]
    for j in range(3):
        vec[:, VEC["cw%d" % j]:VEC["cw%d" % j] + 44] = _chunked(cw[j])
    vec[:, VEC["cb"]:VEC["cb"] + 44] = _chunked(np.asarray(inp["conv_b"], f)[0])
    p = np.arange(128)
    for e in range(2):
        vec[:, VEC["mrow"] + e] = ((p // 16) % 2 == e)
        vec[:, VEC["mpp"] + e] = ((p // 64) == e)
    vec[:, VEC["eps"]] = EPS
    vec[:, VEC["hpi"]] = math.pi / 2
    vec[:, VEC["one"]] = 1.0

    lre = np.asarray(inp["ssm_lambda_re"], f)[0]
    lim = np.asarray(inp["ssm_lambda_im"], f)[0]
    ldt = np.asarray(inp["ssm_log_dt"], f)[0]
    bre = np.asarray(inp["ssm_b_re"], f)[0]
    bim = np.asarray(inp["ssm_b_im"], f)[0]
    cre = np.asarray(inp["ssm_c_re"], f)[0]
    cim = np.asarray(inp["ssm_c_im"], f)[0]
    s5rows = np.zeros((128, 5, 8, 64), f)
    gl = p // 16
    hh = p % 16
    for fc in range(8):
        g = 8 * fc + gl
        s5rows[:, 0, fc, :] = lre[g, :]
        s5rows[:, 1, fc, :] = lim[g, :]
        s5rows[:, 2, fc, :] = ldt[g][:, None]
        s5rows[:, 3, fc, :] = bre[g, :, hh]
        s5rows[:, 4, fc, :] = bim[g, :, hh]
    s5rows = s5rows.reshape(128, 5 * 512)
    e_ = p // 64
    pp = p % 64
    s5pp = np.zeros((128, 96 + 4 * 512), f)
    for q in range(32):
        g = 2 * q + e_
        s5pp[:, q] = lre[g, pp]
        s5pp[:, 32 + q] = lim[g, pp]
        s5pp[:, 64 + q] = ldt[g]
        s5pp[:, 96 + q * 16: 96 + q * 16 + 16] = bre[g, pp, :]
        s5pp[:, 96 + 512 + q * 16: 96 + 512 + q * 16 + 16] = bim[g, pp, :]
        s5pp[:, 96 + 1024 + q * 16: 96 + 1024 + q * 16 + 16] = cre[g, :, pp]
        s5pp[:, 96 + 1536 + q * 16: 96 + 1536 + q * 16 + 16] = cim[g, :, pp]

    cst = np.zeros((128, 256), f)
    cst[:, 0:128] = np.eye(128, dtype=f)
    cst[:, 128:256] = np.triu(np.ones((128, 128), f))

    meta = np.asarray(inp["meta_tokens"], f)
    metaT = np.ascontiguousarray(meta.T.reshape(8, 128, NMETA).transpose(1, 0, 2))
    return dict(wf=wf, vec=vec, s5rows=s5rows, s5pp=s5pp, cst=cst, metaT=metaT)


def build_program(n_seq=SPC, n_tiles_per_seq=SEQ // TT, dbg=None, stop_after=None, ncast=39):
    nc = bass.Bass("TRN2", target_bir_lowering=False)
    xT = nc.dram_tensor("xT", [SPC, 128, 8, SEQ], F32, kind="ExternalInput").ap()
    metaT = nc.dram_tensor("metaT", [128, 8, NMETA], F32, kind="ExternalInput").ap()
    wf = nc.dram_tensor("wf", [39, 128, 4096], F32, kind="ExternalInput").ap()
    vec_d = nc.dram_tensor("vec", [128, NV], F32, kind="ExternalInput").ap()
    s5rows_d = nc.dram_tensor("s5rows", [128, 5 * 512], F32, kind="ExternalInput").ap()
    s5pp_d = nc.dram_tensor("s5pp", [128, 96 + 2048], F32, kind="ExternalInput").ap()
    cst_d = nc.dram_tensor("cst", [128, 256], F32, kind="ExternalInput").ap()
    outT = nc.dram_tensor("outT", [SPC, 128, 8, SEQ], F32, kind="ExternalOutput").ap()
    wb = nc.dram_tensor("wb", [NWT, 128, 4096], BF16, kind="Internal").ap()
    dbg_out = {}
    if dbg:
        for name, shape in dbg.items():
            if name.startswith("_"):
                continue
            dbg_out[name] = nc.dram_tensor("dbg_" + name, list(shape), F32, kind="ExternalOutput").ap()

    P = Prog()
    ARENA_WORDS = 53200

    class Arena:
        def __init__(self, base_ap):
            self.base = base_ap
            self.off = 0

        def reset(self, off=0):
            self.off = off

        def f32(self, n, shape=None):
            a = self.base[:, self.off:self.off + n]
            self.off += n
            assert self.off <= ARENA_WORDS, ("arena overflow", self.off)
            return a

        def bf16(self, n):
            w = (n + 1) // 2
            a = self.base[:, self.off:self.off + w].bitcast(BF16)
            self.off += w
            assert self.off <= ARENA_WORDS, ("arena overflow", self.off)
            return a

    import contextlib
    with contextlib.ExitStack() as es:
        big = es.enter_context(nc.sbuf_tensor("big", [128, ARENA_WORDS], F32))
        ps = es.enter_context(nc.psum_tensor("ps", [128, 7, 512], F32))
        pst = es.enter_context(nc.psum_tensor("pst", [128, 1024], BF16))
        sems = {e: es.enter_context(nc.semaphore("s_" + e)) for e in ENGS}
        sem_w = [es.enter_context(nc.semaphore("w%d" % i)) for i in range(NWB)]
        sem_x = [es.enter_context(nc.semaphore("x%d" % i)) for i in range(2)]
        sem_o = [es.enter_context(nc.semaphore("o%d" % i)) for i in range(2)]
        sem_c = [es.enter_context(nc.semaphore("c%d" % i)) for i in range(4)]
        sem_cv = [es.enter_context(nc.semaphore("cv%d" % i)) for i in range(4)]
        sem_s5 = [es.enter_context(nc.semaphore("s5w%d" % i)) for i in range(2)]
        sem_dbg = [es.enter_context(nc.semaphore("dbgs%d" % i)) for i in range(6)] if dbg else []
        dbg_n = [0]
        block = es.enter_context(nc.Block())

        A = Arena(big[:])
        vec = A.f32(NV)
        cst = A.f32(256)
        ident_bf = A.bf16(128)
        ones_bf = A.bf16(128)
        maskT = cst[:, 128:256]
        scanmsk = A.f32(512)
        lbv = A.f32(24).rearrange("p (a b) -> p a b", a=3)
        ar8 = A.f32(64)
        ai8 = A.f32(64)
        kall = A.bf16(8 * 8 * 128).rearrange("p (f k m) -> p f k m", f=8, k=8)
        RES_END = A.off

        def V(name, i=0, w=1):
            return vec[:, VEC[name] + i: VEC[name] + i + w]

        def pe_mm(out, lhsT, rhs, start, stop, reads, writes, tp=None):
            if tp is None:
                return P.op("pe", lambda e: e.matmul(out, lhsT, rhs, start=start, stop=stop), reads=reads, writes=writes)
            return P.op("pe", lambda e: e.matmul(out, lhsT, rhs, start=start, stop=stop, tile_position=tp),
                        reads=reads, writes=writes)

        def act(out, in_, func, reads, writes, bias=None, scale=1.0):
            if bias is None:
                return P.op("act", lambda e: e.activation(out, in_, func, scale=scale), reads=reads, writes=writes)
            return P.op("act", lambda e: e.activation(out, in_, func, bias=bias, scale=scale), reads=reads, writes=writes)

        def tt(eng, out, in0, in1, op, reads, writes):
            return P.op(eng, lambda e: e.tensor_tensor(out, in0, in1, op), reads=reads, writes=writes)

        def ts(eng, out, in0, s1, s2, op0, op1, reads, writes):
            return P.op(eng, lambda e: e.tensor_scalar(out, in0, s1, s2, op0, op1), reads=reads, writes=writes)

        def ts1(eng, out, in0, s1, op0, reads, writes):
            return P.op(eng, lambda e: e.tensor_single_scalar(out, in0, s1, op0), reads=reads, writes=writes)

        def stt(eng, out, in0, scalar, in1, op0, op1, reads, writes):
            return P.op(eng, lambda e: e.scalar_tensor_tensor(out, in0, scalar, in1, op0, op1), reads=reads, writes=writes)

        def cp(eng, out, in_, reads, writes):
            if eng == "act":
                return P.op("act", lambda e: e.copy(out, in_), reads=reads, writes=writes)
            return P.op(eng, lambda e: e.tensor_copy(out, in_), reads=reads, writes=writes)

        def dma(eng, out, in_, sem, reads, writes, group=False):
            return P.op(eng, lambda e: e.dma_start(out=out, in_=in_), reads=reads, writes=writes, dma_sem=sem, group=group)

        def dbg_dump(name, ap_sb, reads):
            if name in dbg_out:
                dma("pool", dbg_out[name], ap_sb, sem_dbg[dbg_n[0]], reads, [("dbg", name)])
                dbg_n[0] += 1

        dma("sp", vec, vec_d, sem_c[0], [], ["vec"])
        dma("sp", cst, cst_d, sem_c[1], [], ["cst"])
        for t in range(ncast):
            dma("pool", wb[t], wf[t], sem_cv[t % 4], [], [("wb", t)])
        cp("dve", ident_bf, cst[:, 0:128], ["cst"], ["ident"])
        P.op("dve", lambda e: e.memset(ones_bf, 1.0), writes=["ones"])
        P.op("dve", lambda e: e.memset(scanmsk, 1.0), writes=["scanmsk"])
        P.op("dve", lambda e: e.memset(scanmsk.rearrange("p (c j) -> p c j", j=128)[:, :, 0:1], 0.0),
             reads=["scanmsk"], writes=["scanmsk"])
        tt("dve", lbv[:, 0, :], V("lbl0", 0, 8), V("lbl1", 0, 8), ALU.subtract, ["vec"], ["lbv"])
        act(lbv[:, 0, :], lbv[:, 0, :], AF.Sigmoid, ["lbv"], ["lbv"])
        ts("dve", lbv[:, 1, :], lbv[:, 0, :], -1.0, 1.0, ALU.mult, ALU.add, ["lbv"], ["lbv"])
        ts1("dve", lbv[:, 2, :], lbv[:, 1, :], -1.0, ALU.mult, ["lbv"], ["lbv"])

        A.reset(RES_END)

        def coef_chain(lre, lim, ldt, F, alloc, key, nw, na, on_w, on_a):
            R = [key]
            dt = alloc(F); x = alloc(F); th = alloc(F); mag = alloc(F); c = alloc(F); s_ = alloc(F)
            t1 = alloc(F); t2 = alloc(F); t3 = alloc(F)
            act(dt, ldt, AF.Exp, R, R)
            tt("dve", x, lre, dt, ALU.mult, R, R)
            tt("dve", th, lim, dt, ALU.mult, R, R)
            act(mag, x, AF.Exp, R, R)
            ts1("dve", x, th, 1.0 / 16.0, ALU.mult, R, R)
            tt("dve", t1, x, x, ALU.mult, R, R)
            sc_ = [1.0 / 6227020800.0, -1.0 / 39916800.0, 1.0 / 362880.0, -1.0 / 5040.0, 1.0 / 120.0, -1.0 / 6.0]
            ts1("dve", t2, t1, sc_[0], ALU.mult, R, R)
            for cf in sc_[1:]:
                stt("dve", t2, t2, cf, t1, ALU.add, ALU.mult, R, R)
            stt("dve", s_, t2, 1.0, x, ALU.add, ALU.mult, R, R)
            cc_ = [-1.0 / 87178291200.0, 1.0 / 479001600.0, -1.0 / 3628800.0, 1.0 / 40320.0, -1.0 / 720.0,
                   1.0 / 24.0, -0.5]
            ts1("dve", t3, t1, cc_[0], ALU.mult, R, R)
            for cf in cc_[1:]:
                stt("dve", t3, t3, cf, t1, ALU.add, ALU.mult, R, R)
            ts1("dve", c, t3, 1.0, ALU.add, R, R)
            for _ in range(4):
                tt("dve", t1, c, c, ALU.mult, R, R)
                tt("dve", t2, s_, s_, ALU.mult, R, R)
                tt("dve", t3, c, s_, ALU.mult, R, R)
                tt("dve", c, t1, t2, ALU.subtract, R, R)
                ts1("dve", s_, t3, 2.0, ALU.mult, R, R)
            tt("dve", t1, c, c, ALU.mult, R, R)
            tt("dve", t2, s_, s_, ALU.mult, R, R)
            tt("dve", t1, t1, t2, ALU.add, R, R)
            ts("dve", t1, t1, -0.5, 1.5, ALU.mult, ALU.add, R, R)
            tt("dve", c, c, t1, ALU.mult, R, R)
            tt("dve", s_, s_, t1, ALU.mult, R, R)
            ar = alloc(F); ai = alloc(F)
            tt("dve", ar, mag, c, ALU.mult, R, R)
            tt("dve", ai, mag, s_, ALU.mult, R, R)
            den = dt
            tt("dve", t1, lre, lre, ALU.mult, R, R)
            tt("dve", t2, lim, lim, ALU.mult, R, R)
            tt("dve", den, t1, t2, ALU.add, R, R)
            P.op("dve", lambda e: e.reciprocal(den, den), reads=R, writes=R)
            nr = mag
            ts1("dve", nr, ar, -1.0, ALU.add, R, R)
            w0r = alloc(F); w0i = alloc(F); w1r = alloc(F); w1i = alloc(F)
            tt("dve", t1, nr, lre, ALU.mult, R, R)
            tt("dve", t2, ai, lim, ALU.mult, R, R)
            tt("dve", t1, t1, t2, ALU.add, R, R)
            tt("dve", w0r, t1, den, ALU.mult, R, R)
            tt("dve", t1, ai, lre, ALU.mult, R, R)
            tt("dve", t2, nr, lim, ALU.mult, R, R)
            tt("dve", t1, t1, t2, ALU.subtract, R, R)
            tt("dve", w0i, t1, den, ALU.mult, R, R)

            def cmul(o_r, o_i, xr, xi, yr, yi):
                tt("dve", t1, xr, yr, ALU.mult, R, R)
                tt("dve", t2, xi, yi, ALU.mult, R, R)
                tt("dve", o_r, t1, t2, ALU.subtract, R, R)
                tt("dve", t1, xr, yi, ALU.mult, R, R)
                tt("dve", t2, xi, yr, ALU.mult, R, R)
                tt("dve", o_i, t1, t2, ALU.add, R, R)
            cur = (w0r, w0i)
            nxt = (w1r, w1i)
            for k in range(nw):
                on_w(k, cur[0], cur[1], (x, th, t3))
                if k + 1 < nw:
                    cmul(nxt[0], nxt[1], cur[0], cur[1], ar, ai)
                    cur, nxt = nxt, cur
            if na >= 1:
                p0 = (c, s_)
                p1 = (w0r, w0i) if nw == 0 else cur
                p1 = nxt
                on_a(1, ar, ai, (x, th, t3))
                curp = (ar, ai)
                bufs = [p0, p1]
                for k in range(2, na + 1):
                    dst = bufs[k % 2]
                    cmul(dst[0], dst[1], curp[0], curp[1], ar, ai)
                    curp = dst
                    on_a(k, curp[0], curp[1], (x, th, t3))

        rin = A.f32(5 * 512)
        dma("sp", rin, s5rows_d, sem_c[2], [], ["rows"])
        rv = rin.rearrange("p (a f) -> p a f", a=5)
        eall = A.bf16(8 * 8 * 2 * 2 * 64).rearrange("p (f i r e s) -> p f i r e s", f=8, i=8, r=2, e=2)
        bre_r = rv[:, 3, :]
        bim_r = rv[:, 4, :]

        def on_w_rows(k, wr, wi, tmp):
            i = 7 - k
            t1r, t2r, t3r = tmp
            R = ["rows"]
            tt("dve", t1r, wr, bre_r, ALU.mult, R, R)
            tt("dve", t3r, wi, bim_r, ALU.mult, R, R)
            tt("dve", t1r, t1r, t3r, ALU.subtract, R, R)
            tt("dve", t2r, wr, bim_r, ALU.mult, R, R)
            tt("dve", t3r, wi, bre_r, ALU.mult, R, R)
            tt("dve", t2r, t2r, t3r, ALU.add, R, R)
            for ri, src in ((0, t1r), (1, t2r)):
                for e in range(2):
                    ts1("pool" if e else "dve", eall[:, :, i, ri, e, :], src.rearrange("p (f s) -> p f s", f=8),
                        V("mrow", e), ALU.mult, R + ["vec"], ["eall"])
        coef_chain(rv[:, 0, :], rv[:, 1, :], rv[:, 2, :], 512, A.f32, "rows", 8, 0, on_w_rows, None)
        eflat = eall.rearrange("p f i r e s -> p (f i r e s)")
        for t in range(4):
            dma("sp", wb[39 + t], eflat[:, t * 4096:(t + 1) * 4096], sem_s5[0], ["eall"], [("wb", 39 + t)], group=True)
        P.fence()
        A.reset(RES_END)

        pin = A.f32(96 + 2048)
        dma("sp", pin, s5pp_d, sem_c[3], [], ["pp"])
        small = A.f32(32 * 64)
        so = [0]

        def alloc_small(F):
            a = small[:, so[0]:so[0] + F]
            so[0] += F
            assert so[0] <= 32 * 64
            return a
        bre_p = pin[:, 96:96 + 512].rearrange("p (q h) -> p q h", q=32)
        bim_p = pin[:, 96 + 512:96 + 1024].rearrange("p (q h) -> p q h", q=32)
        cre_p = pin[:, 96 + 1024:96 + 1536].rearrange("p (q h) -> p q h", q=32)
        cim_p = pin[:, 96 + 1536:96 + 2048].rearrange("p (q h) -> p q h", q=32)
        u1 = A.f32(512).rearrange("p (q h) -> p q h", q=32)
        u2 = A.f32(512).rearrange("p (q h) -> p q h", q=32)
        u3 = A.f32(512).rearrange("p (q h) -> p q h", q=32)

        def bcq(a_):
            return a_.unsqueeze(2).to_broadcast([128, 32, 16])
        gall = A.bf16(32 * 8 * 2 * 2 * 16).rearrange("p (q j r e h) -> p q j r e h", q=32, j=8, r=2, e=2)
        lk = A.bf16(8 * 2 * 32 * 32).rearrange("p (k r q e h) -> p k r q e h", k=8, r=2, q=32, e=2)
        R = ["pp"]

        def on_w_pp(k, wr, wi, tmp):
            tt("dve", u1, bre_p, bcq(wr), ALU.mult, R, R)
            tt("dve", u3, bim_p, bcq(wi), ALU.mult, R, R)
            tt("dve", u1, u1, u3, ALU.subtract, R, R)
            tt("dve", u2, bim_p, bcq(wr), ALU.mult, R, R)
            tt("dve", u3, bre_p, bcq(wi), ALU.mult, R, R)
            tt("dve", u2, u2, u3, ALU.add, R, R)
            for ri, src in ((0, u1), (1, u2)):
                for e in range(2):
                    ts1("pool" if e else "dve", lk[:, k, ri, :, e, :], src, V("mpp", e), ALU.mult,
                        R + ["vec"], ["lk"])

        def on_a_pp(k, a_r, a_i, tmp):
            j = k - 1
            tt("dve", u1, cre_p, bcq(a_r), ALU.mult, R, R)
            tt("dve", u3, cim_p, bcq(a_i), ALU.mult, R, R)
            tt("dve", u1, u1, u3, ALU.subtract, R, R)
            tt("dve", u2, cre_p, bcq(a_i), ALU.mult, R, R)
            tt("dve", u3, cim_p, bcq(a_r), ALU.mult, R, R)
            tt("dve", u2, u2, u3, ALU.add, R, R)
            ts1("dve", u2, u2, -1.0, ALU.mult, R, R)
            for ri, src in ((0, u1), (1, u2)):
                for e in range(2):
                    ts1("pool" if e else "dve", gall[:, :, j, ri, e, :], src, V("mpp", e), ALU.mult,
                        R + ["vec"], ["gall"])
            if k == 8:
                for h2 in range(2):
                    cp("dve", ar8[:, h2 * 32:(h2 + 1) * 32], a_r, ["pp"], ["a8"])
                    cp("dve", ai8[:, h2 * 32:(h2 + 1) * 32], a_i, ["pp"], ["a8"])
        coef_chain(pin[:, 0:32], pin[:, 32:64], pin[:, 64:96], 32, alloc_small, "pp", 8, 8, on_w_pp, on_a_pp)
        gflat = gall.rearrange("p q j r e h -> p (q j r e h)")
        for t in range(4):
            dma("sp", wb[43 + t], gflat[:, t * 4096:(t + 1) * 4096], sem_s5[1], ["gall"], [("wb", 43 + t)], group=True)
        cpad = A.bf16(2 * 32 * 128).rearrange("p (r f l g h) -> p r f l g h", r=2, f=8, l=4, g=8)
        P.op("pool", lambda e: e.memset(cpad, 0.0), writes=["cpad"])
        ts1("dve", u2, cim_p, -1.0, ALU.mult, R, R)
        for ri, src in ((0, cre_p), (1, u2)):
            srcv = src.rearrange("p (f l) h -> p f l h", f=8)
            for ql in range(4):
                for e in range(2):
                    ts1("dve", cpad[:, ri, :, ql, 2 * ql + e, :], srcv[:, :, ql, :], V("mpp", e), ALU.mult,
                        R + ["vec", "cpad"], ["cpad"])
        ddiag = A.f32(128 * 8).rearrange("p (f m) -> p f m", f=8)
        for fc in range(8):
            ts1("dve", ddiag[:, fc, :], cst[:, 0:128], V("d", fc), ALU.mult, ["cst", "vec"], ["ddiag"])
        kb = [0]
        for fc in range(8):
            for k in range(8):
                bank = kb[0] % 7
                kb[0] += 1
                for ql in range(4):
                    q = 4 * fc + ql
                    for ri in range(2):
                        pe_mm(ps[32 * ql:32 * ql + 32, bank, 0:128],
                              lk[:, k, ri, q, :, :].rearrange("p e h -> p (e h)"),
                              cpad[:, ri, fc, ql, :, :].rearrange("p g h -> p (g h)"),
                              ri == 0, ri == 1, ["lk", "cpad"], [("ps", bank)], tp=(0, 32 * ql))
                if k == 0:
                    tt("dve", kall[:, fc, k, :], ps[:, bank, 0:128], ddiag[:, fc, :], ALU.add,
                       [("ps", bank), "ddiag"], ["kall"])
                else:
                    cp("act" if (k % 2) else "dve", kall[:, fc, k, :], ps[:, bank, 0:128], [("ps", bank)], ["kall"])
        if "kall" in dbg_out:
            kf = A.f32(8 * 8 * 128)
            cp("dve", kf, kall.rearrange("p f k m -> p (f k m)"), ["kall"], ["kf"])
            dbg_dump("kall", kf, ["kf"])

        P.fence()

        A.reset(RES_END)
        HB_ = [A.f32(8 * 512).rearrange("p (f t) -> p f t", f=8) for _ in range(2)]
        X = [A.bf16(8 * 512).rearrange("p (f t) -> p f t", f=8) for _ in range(7)]
        sall = A.f32(65 * 64).rearrange("p (c s) -> p c s", c=65)
        sbf = A.bf16(64 * 64).rearrange("p (s c) -> p s c", s=64)
        NFS = 6
        FS = [A.f32(516) for _ in range(NFS)]
        WBUF = [A.bf16(4096) for _ in range(NWB)]
        st = A.f32(8 * 128).rearrange("p (h v) -> p h v", h=8)
        stm = A.f32(8 * 128).rearrange("p (h v) -> p h v", h=8)
        stb = A.bf16(2 * 4 * 128).rearrange("p (r b v) -> p r b v", r=2, b=4)
        scm = A.bf16(2 * 512).rearrange("p (r t) -> p r t", r=2)
        osq = A.bf16(512)
        sog = A.bf16(512)
        kot = A.bf16(512)
        gsg = A.bf16(4 * 512).rearrange("p (m t) -> p m t", m=4)
        dec = A.f32(32).rearrange("p (h b) -> p h b", h=8)
        smeta = A.f32(64)
        sct = A.f32(128).rearrange("p (a s) -> p a s", a=2)
        uph = A.f32(88).rearrange("p (c j) -> p c j", c=44)
        uphm = A.f32(88).rearrange("p (c j) -> p c j", c=44)
        print("arena used words", A.off, "of", ARENA_WORDS)

        zb, X1, X2, X3, X4, X5, X6 = X
        vt = X4.rearrange("p f t -> p (f t)").rearrange("p (b v) -> p b v", b=4)
        kout = X5.rearrange("p f t -> p (f t)").rearrange("p (b h d) -> p b h d", b=4, h=8)

        P.op("dve", lambda e: e.memset(sall[:, 0, :], 0.0), writes=[("sall", 0)])
        P.op("dve", lambda e: e.memset(st, 0.0), writes=[("st", h) for h in range(8)])
        P.op("dve", lambda e: e.memset(uph, 0.0), writes=["uph"])

        pb = [0]

        def pbank():
            b = pb[0] % 7
            pb[0] += 1
            return b
        fsn = [0]

        def fs():
            i = fsn[0] % NFS
            fsn[0] += 1
            return FS[i], ("fs", i)

        per_tile = ([0, 1, 39, 40, 41, 42, 2, 3, 4, 5, 6, 7, 8, 9, 43, 44, 45, 46, 14, 15,
                     10, 16, 11, 17, 12, 18, 13, 19, 20, 21]
                    + [22, 23, 24, 25, 33, 34, 26, 27, 28, 29, 35, 36, 30, 31, 32, 37, 38])
        n_tok_tiles = 1 + n_seq * n_tiles_per_seq
        wseq = per_tile * n_tok_tiles
        wstate = {"issued": 0, "next": 0}

        def w_issue(upto):
            while wstate["issued"] < min(upto, len(wseq)):
                n = wstate["issued"]
                slot = n % NWB
                tid = wseq[n]
                dma("sp", WBUF[slot], wb[tid], sem_w[slot], [("wb", tid)], [("wbuf", slot)])
                wstate["issued"] += 1

        def wget(expect):
            n = wstate["next"]
            assert wseq[n] == expect, (n, wseq[n], expect)
            w_issue(n + NWB)
            wstate["next"] += 1
            slot = n % NWB
            return WBUF[slot], ("wbuf", slot)

        def load_h(tile, Hbuf, hkey, par):
            kind, b, t0, Tn = tile
            if kind == "meta":
                dma("sp", Hbuf[:, :, 0:Tn], metaT, sem_x[par], [], [(hkey, fc) for fc in range(8)])
            else:
                dma("sp", Hbuf[:, :, 0:Tn], xT[b, :, :, t0:t0 + Tn], sem_x[par], [], [(hkey, fc) for fc in range(8)])

        def rmsnorm(Hbuf, hkey, gname, Tn, out_buf, okey, sqbuf, sqkey, inplace=False):
            for fc in range(8):
                act(sqbuf[:, fc, :Tn], Hbuf[:, fc, :Tn], AF.Square, [(hkey, fc)], [(sqkey, fc)])
            bank = pbank()
            for fc in range(8):
                pe_mm(ps[:, bank, :Tn], ones_bf, sqbuf[:, fc, :Tn], fc == 0, fc == 7,
                      ["ones", (sqkey, fc)], [("ps", bank)])
            rs, rk = fs()
            act(rs[:, :Tn], ps[:, bank, :Tn], AF.Sqrt, [("ps", bank), "vec"], [rk], bias=V("eps"), scale=1.0 / D)
            P.op("dve", lambda e: e.reciprocal(rs[:, :Tn], rs[:, :Tn]), reads=[rk], writes=[rk])
            for fc in range(8):
                stt("dve", out_buf[:, fc, :Tn], Hbuf[:, fc, :Tn], V(gname, fc), rs[:, :Tn], ALU.mult, ALU.mult,
                    [(hkey, fc), rk, "vec"], [(okey, fc)])

        def proj_fm(wt, wkey, rhs_buf, rkey, Tn, evac, nk=8):
            for mi in range(4):
                bank = pbank()
                for kc in range(nk):
                    pe_mm(ps[:, bank, :Tn], wt[:, kc * 512 + mi * 128: kc * 512 + mi * 128 + 128],
                          rhs_buf[:, kc, :Tn], kc == 0, kc == nk - 1, [wkey, (rkey, kc)], [("ps", bank)])
                evac(mi, bank)

        tile_ctr = [0]

        def stage(name):
            if stop_after is None:
                return
            if stop_after == name or (tile_ctr[0] >= 2 and stop_after == "main:" + name):
                raise StopBuild()

        def do_tile(tile, Hbuf, hkey, Ebuf, ekey, next_load):
            kind, b, t0, Tn = tile
            is_meta = kind == "meta"
            NCH = Tn // 8
            bs = min(128, Tn)
            nblk = Tn // bs
            tix = tile_ctr[0]
            tile_ctr[0] += 1
            want_dbg = dbg is not None and tix == dbg.get("_tile", -1)

            rmsnorm(Hbuf, hkey, "g1", Tn, zb, "z", X1, "x1")
            stage("norm1")
            for t in range(2):
                wt, wk = wget(t)

                def ev_u(mi, bank, t=t):
                    m = 4 * t + mi
                    cp("act" if mi % 2 == 0 else "dve", X1[:, m, :Tn], ps[:, bank, :Tn], [("ps", bank)], [("x1", m)])
                proj_fm(wt, wk, zb, "z", Tn, ev_u)
            stage("uproj")
            for t in range(4):
                wt, wk = wget(39 + t)
                wE = wt.rearrange("p (f i r m) -> p f i r m", f=2, i=8, r=2)
                for fcl in range(2):
                    fc = 2 * t + fcl
                    banks = [pbank() for _ in range(4)]
                    uv = X1[:, fc, :Tn].rearrange("p (c j) -> p c j", j=8)
                    for ri in range(2):
                        for i in range(8):
                            for ql in range(4):
                                pe_mm(ps[:, banks[ql], ri * NCH:(ri + 1) * NCH], wE[32 * ql:32 * ql + 32, fcl, i, ri, :],
                                      uv[32 * ql:32 * ql + 32, :, i], i == 0, i == 7,
                                      [wk, ("x1", fc)], [("ps", banks[ql])], tp=(32 * ql, 0))
                    for ql in range(4):
                        q = 4 * fc + ql
                        dst = sall[:, 1:1 + NCH, :].rearrange("p c (r q) -> p q r c", r=2)[:, q, :, :]
                        src = ps[:, banks[ql], 0:2 * NCH].rearrange("p (r c) -> p r c", r=2)
                        cp("act" if ql % 2 == 0 else "dve", dst, src, [("ps", banks[ql])],
                           [("sall", 1 + c) for c in range(NCH)])
            stage("eproj")
            SE = SCAN_ENG
            scan_pos = [1]

            def scan_some(nsteps):
                for _ in range(nsteps):
                    c = scan_pos[0]
                    if c > NCH:
                        return
                    scan_pos[0] += 1
                    prev = sall[:, c - 1, :]
                    cur = sall[:, c, :]
                    kp = ("sall", c - 1)
                    kc_ = ("sall", c)
                    tt(SE, sct[:, 0, :], prev, ar8, ALU.mult, [kp, "a8"], ["sct0"])
                    tt(SE, sct[:, 1, :], prev, ai8, ALU.mult, [kp, "a8"], ["sct1"])
                    tt(SE, sct[:, 0, :], sct[:, 0, :], cur, ALU.add, ["sct0", kc_], ["sct0"])
                    tt(SE, cur[:, 0:32], sct[:, 0, 0:32], sct[:, 1, 32:64], ALU.subtract, ["sct0", "sct1"], [kc_])
                    tt(SE, cur[:, 32:64], sct[:, 0, 32:64], sct[:, 1, 0:32], ALU.add, ["sct0", "sct1", kc_], [kc_])

            def scan_finish():
                scan_some(NCH)
                cp("dve", sbf[:, :, 0:NCH], sall[:, 0:NCH, :].rearrange("p c s -> p s c"),
                   [("sall", c) for c in range(NCH)], ["sbf"])
                if is_meta:
                    cp("dve", smeta, sall[:, NCH, :], [("sall", NCH)], ["smeta"])
                else:
                    cp("dve", sall[:, 0, :], sall[:, NCH, :], [("sall", NCH), "sbf"], [("sall", 0)])
            stage("scan")
            for t in range(2):
                wt, wk = wget(2 + t)

                def ev_f(mi, bank, t=t):
                    hd = 4 * t + mi
                    g, gk = fs()
                    lf, lk_ = fs()
                    kk, kkk = fs()
                    cm, cmk = fs()
                    act(g[:, :Tn], ps[:, bank, :Tn], AF.Sigmoid, [("ps", bank)], [gk])
                    act(lf[:, :Tn], g[:, :Tn], AF.Ln, [gk, "lbv"], [lk_], bias=lbv[:, 0, hd:hd + 1], scale=lbv[:, 1, hd:hd + 1])
                    ts("dve", kk[:, :Tn], g[:, :Tn], lbv[:, 2, hd:hd + 1], lbv[:, 1, hd:hd + 1], ALU.mult, ALU.add,
                       [gk, "lbv"], [kkk])
                    P.op("dve", lambda e: e.tensor_tensor_scan(cm[:, :Tn], scanmsk[:, :Tn], lf[:, :Tn], 0.0, ALU.mult, ALU.add),
                         reads=["scanmsk", lk_], writes=[cmk])
                    act(Ebuf[:, hd, :Tn], cm[:, :Tn], AF.Exp, [cmk], [(ekey, hd)])
                    act(lf[:, :Tn], cm[:, :Tn], AF.Exp, [cmk, lk_], [lk_], scale=-1.0)
                    tt("dve", kk[:, :Tn], kk[:, :Tn], lf[:, :Tn], ALU.mult, [kkk, lk_], [kkk])
                    cp("act", X2[:, hd, :Tn], kk[:, :Tn], [kkk], [("x2", hd)])
                    cp("dve", dec[:, hd, 0:nblk], Ebuf[:, hd, :Tn].rearrange("p (b j) -> p b j", j=bs)[:, :, bs - 1],
                       [(ekey, hd)], [("dec", hd)])
                    tt("dve", kot[:, :Tn].rearrange("p (b j) -> p b j", j=bs), kk[:, :Tn].rearrange("p (b j) -> p b j", j=bs),
                       dec[:, hd, 0:nblk].unsqueeze(2).to_broadcast([128, nblk, bs]), ALU.mult,
                       [kkk, ("dec", hd)], ["kot"])
                    half = hd % 2
                    for blk in range(nblk):
                        P.op("pe", lambda e, blk=blk: e.transpose(pst[0:bs, half * 512 + blk * 128: half * 512 + blk * 128 + 128],
                                                                  kot[:, blk * bs:(blk + 1) * bs], ident_bf),
                             reads=["kot", "ident"], writes=[("pst", half)])
                    cp("act", kout[0:bs, 0:nblk, hd, :],
                       pst[0:bs, half * 512: half * 512 + nblk * 128].rearrange("p (b d) -> p b d", b=nblk),
                       [("pst", half)], [("x5", hd)])
                    scan_some(8)
                proj_fm(wt, wk, zb, "z", Tn, ev_f)
            scan_finish()
            stage("fproj")
            for t in range(2):
                wt, wk = wget(4 + t)

                def ev_q(mi, bank, t=t):
                    hd = 4 * t + mi
                    tt("dve", X3[:, hd, :Tn], ps[:, bank, :Tn], Ebuf[:, hd, :Tn], ALU.mult,
                       [("ps", bank), (ekey, hd)], [("x3", hd)])
                proj_fm(wt, wk, zb, "z", Tn, ev_q)
            stage("qproj")
            for t in range(2):
                wt, wk = wget(6 + t)
                for blk in range(nblk):
                    bank = pbank()
                    for kc in range(8):
                        pe_mm(ps[0:bs, bank, :], zb[:, kc, blk * bs:(blk + 1) * bs], wt[:, kc * 512:(kc + 1) * 512],
                              kc == 0, kc == 7, [wk, ("z", kc)], [("ps", bank)])
                    cp("act" if blk % 2 == 0 else "dve", vt[0:bs, blk, t * 512:(t + 1) * 512], ps[0:bs, bank, :],
                       [("ps", bank)], [("x4", 2 * blk + t)])
            stage("iproj")
            for hd in range(8):
                if hd % 4 == 0:
                    wog, wogk = wget(8 + hd // 4)
                rot = hd % 2
                bS = pbank()
                for blk in range(nblk):
                    tok = slice(blk * bs, (blk + 1) * bs)
                    pe_mm(ps[0:bs, bS, blk * 128: blk * 128 + bs], X2[:, hd, tok], X3[:, hd, tok], True, True,
                          [("x2", hd), ("x3", hd)], [("ps", bS)])
                tt("dve", scm[0:bs, rot, 0:nblk * bs].rearrange("p (b t) -> p b t", b=nblk),
                   ps[0:bs, bS, 0:nblk * 128].rearrange("p (b t) -> p b t", b=nblk)[:, :, 0:bs],
                   maskT[0:bs, 0:bs].unsqueeze(1).to_broadcast([bs, nblk, bs]), ALU.mult,
                   [("ps", bS), "cst"], [("scm", rot)])
                bU = pbank()
                for blk in range(nblk):
                    pe_mm(ps[:, bU, blk * 128:(blk + 1) * 128], kout[0:bs, blk, hd, :], vt[0:bs, blk, hd * 128:(hd + 1) * 128],
                          True, True, [("x5", hd), ("x4", 2 * blk + hd // 4)], [("ps", bU)])
                for blk in range(nblk):
                    cp("act", stb[:, rot, blk, :], st[:, hd, :], [("st", hd)], [("stb", rot, blk)])
                    stt("dve", st[:, hd, :], st[:, hd, :], dec[:, hd, blk:blk + 1], ps[:, bU, blk * 128:(blk + 1) * 128],
                        ALU.mult, ALU.add, [("st", hd), ("dec", hd), ("ps", bU)], [("st", hd)])
                bO = pbank()
                for blk in range(nblk):
                    tok = slice(blk * bs, (blk + 1) * bs)
                    pe_mm(ps[:, bO, blk * bs:(blk + 1) * bs], vt[0:bs, blk, hd * 128:(hd + 1) * 128],
                          scm[0:bs, rot, blk * bs:(blk + 1) * bs], True, False,
                          [("x4", 2 * blk + hd // 4), ("scm", rot)], [("ps", bO)])
                    pe_mm(ps[:, bO, blk * bs:(blk + 1) * bs], stb[:, rot, blk, :], X3[:, hd, tok], False, True,
                          [("stb", rot, blk), ("x3", hd)], [("ps", bO)])
                osb, osk = fs()
                cp("dve", osb[:, :Tn], ps[:, bO, :Tn], [("ps", bO)], [osk])
                act(osq[:, :Tn], osb[:, :Tn], AF.Square, [osk], ["osq"])
                bN = pbank()
                pe_mm(ps[:, bN, :Tn], ones_bf, osq[:, :Tn], True, True, ["ones", "osq"], [("ps", bN)])
                rs, rk = fs()
                act(rs[:, :Tn], ps[:, bN, :Tn], AF.Sqrt, [("ps", bN), "vec"], [rk], bias=V("eps"), scale=1.0 / 128)
                P.op("dve", lambda e, rs=rs: e.reciprocal(rs[:, :Tn], rs[:, :Tn]), reads=[rk], writes=[rk])
                bG = pbank()
                mi = hd % 4
                for kc in range(8):
                    pe_mm(ps[:, bG, :Tn], wog[:, kc * 512 + mi * 128: kc * 512 + mi * 128 + 128], zb[:, kc, :Tn],
                          kc == 0, kc == 7, [wogk, ("z", kc)], [("ps", bG)])
                act(sog[:, :Tn], ps[:, bG, :Tn], AF.Silu, [("ps", bG)], ["sog"])
                stt("dve", osb[:, :Tn], osb[:, :Tn], V("ng"), rs[:, :Tn], ALU.mult, ALU.mult, [osk, rk, "vec"], [osk])
                tt("dve", X6[:, hd, :Tn], osb[:, :Tn], sog[:, :Tn], ALU.mult, [osk, "sog"], [("x6", hd)])
            if is_meta:
                cp("dve", stm, st, [("st", h) for h in range(8)], ["stm"])
            if next_load is not None:
                next_load()
            if want_dbg:
                dbg_dump("yb", X6.rearrange("p f t -> p (f t)"), [("x6", h) for h in range(8)])
                dbg_dump("uT", X1.rearrange("p f t -> p (f t)"), [("x1", h) for h in range(8)])

            stage("hgrn")
            for fc in range(8):
                if fc % 2 == 0:
                    wG_, wGk = wget(43 + fc // 2)
                    wG = wG_.rearrange("p (q j r m) -> p q j r m", q=8, j=8, r=2)
                bY = pbank()
                yv = ps[:, bY, 0:Tn].rearrange("p (c j) -> p c j", j=8)
                uv = X1[:, fc, :Tn].rearrange("p (c j) -> p c j", j=8)
                for k in range(8):
                    pe_mm(yv[:, :, k:8], kall[:, fc, k, :], uv[:, :, 0:8 - k], k == 0, False,
                          ["kall", ("x1", fc)], [("ps", bY)])
                for ql in range(4):
                    q = 4 * fc + ql
                    q8 = q % 8
                    for j in range(8):
                        for ri in range(2):
                            last = (j == 7 and ri == 1)
                            pe_mm(yv[32 * ql:32 * ql + 32, :, j], wG[:, q8, j, ri, :], sbf[:, ri * 32 + q, 0:NCH],
                                  False, last, [wGk, "sbf"], [("ps", bY)], tp=(0, 32 * ql))
                if USE_GELU_LUT:
                    act(X4[:, fc, :Tn], ps[:, bY, :Tn], AF.Gelu_apprx_tanh, [("ps", bY)], [("x4", fc)])
                else:
                    xs, xk = fs()
                    x2, x2k = fs()
                    cp("act", xs[:, :Tn], ps[:, bY, :Tn], [("ps", bY)], [xk])
                    act(x2[:, :Tn], ps[:, bY, :Tn], AF.Square, [("ps", bY)], [x2k])
                    ts("dve", x2[:, :Tn], x2[:, :Tn], 0.044715, 1.0, ALU.mult, ALU.add, [x2k], [x2k])
                    tt("dve", x2[:, :Tn], x2[:, :Tn], xs[:, :Tn], ALU.mult, [x2k, xk], [x2k])
                    act(x2[:, :Tn], x2[:, :Tn], AF.Sigmoid, [x2k], [x2k], scale=1.5957691216057308)
                    tt("dve", X4[:, fc, :Tn], xs[:, :Tn], x2[:, :Tn], ALU.mult, [xk, x2k], [("x4", fc)])
            if want_dbg:
                dbg_dump("ya", X4.rearrange("p f t -> p (f t)"), [("x4", h) for h in range(8)])
            stage("s5out")
            for t in range(2):
                wt, wk = wget(14 + t)

                def ev_glu(mi, bank, t=t):
                    m = 4 * t + mi
                    sg, sgk = fs()
                    act(sg[:, :Tn], ps[:, bank, :Tn], AF.Sigmoid, [("ps", bank)], [sgk])
                    tt("dve", X1[:, m, :Tn], X4[:, m, :Tn], sg[:, :Tn], ALU.mult,
                       [("x4", m), sgk], [("x1", m)])
                proj_fm(wt, wk, X4, "x4", Tn, ev_glu)
            stage("glu")
            for t in range(2):
                wt, wk = wget(10 + t)

                def ev_ga(mi, bank):
                    act(gsg[:, mi, :Tn], ps[:, bank, :Tn], AF.Sigmoid, [("ps", bank)], [("gsg", mi)])
                proj_fm(wt, wk, zb, "z", Tn, ev_ga)
                wt, wk = wget(16 + t)

                def ev_sp(mi, bank, t=t):
                    m = 4 * t + mi
                    tt("dve", X2[:, m, :Tn], ps[:, bank, :Tn], gsg[:, mi, :Tn], ALU.mult,
                       [("ps", bank), ("gsg", mi)], [("x2", m)])
                proj_fm(wt, wk, X1, "x1", Tn, ev_sp)
            for t in range(2):
                wt, wk = wget(12 + t)
                proj_fm(wt, wk, zb, "z", Tn, ev_ga)
                wt, wk = wget(18 + t)

                def ev_hp(mi, bank, t=t):
                    m = 4 * t + mi
                    tmp, tk = fs()
                    tt("dve", tmp[:, :Tn], ps[:, bank, :Tn], gsg[:, mi, :Tn], ALU.mult,
                       [("ps", bank), ("gsg", mi)], [tk])
                    tt("dve", X3[:, m, :Tn], tmp[:, :Tn], X2[:, m, :Tn], ALU.add, [tk, ("x2", m)], [("x3", m)])
                proj_fm(wt, wk, X6, "x6", Tn, ev_hp)
            stage("merge")
            for t in range(2):
                wt, wk = wget(20 + t)

                def ev_o(mi, bank, t=t):
                    m = 4 * t + mi
                    tt("dve", Hbuf[:, m, :Tn], ps[:, bank, :Tn], Hbuf[:, m, :Tn], ALU.add,
                       [("ps", bank), (hkey, m)], [(hkey, m)])
                proj_fm(wt, wk, X3, "x3", Tn, ev_o)
            if want_dbg:
                dbg_dump("h1", Hbuf.rearrange("p f t -> p (f t)"), [(hkey, h) for h in range(8)])
            stage("wout")
            rmsnorm(Hbuf, hkey, "g2", Tn, zb, "z", X1, "x1")
            stage("norm2")
            for r in range(3):
                tiles_r = list(range(4 * r, min(4 * r + 4, 11)))
                for i in tiles_r:
                    wt, wk = wget(22 + i)
                    banks = []
                    cvals = []

                    def ev_up(mi, bank, i=i, cvals=cvals):
                        h2 = mi // 2
                        j = 2 * i + h2
                        ci = j if mi % 2 == 0 else 22 + j
                        cin, ck = fs()
                        cc, cck = fs()
                        cp("act", cin[:, 2:2 + Tn], ps[:, bank, :Tn], [("ps", bank)], [ck])
                        ts("dve", cc[:, :Tn], cin[:, 2:2 + Tn], V("cw2", ci), V("cb", ci), ALU.mult, ALU.add,
                           [ck, "vec"], [cck])
                        cp("act", cin[:, 0:2], uph[:, ci, :], [("uph", ci), ck], [ck])
                        cp("act", uph[:, ci, :], cin[:, Tn:Tn + 2], [ck], [("uph", ci)])
                        stt("dve", cc[:, :Tn], cin[:, 1:1 + Tn], V("cw1", ci), cc[:, :Tn], ALU.mult, ALU.add,
                            [ck, cck, "vec"], [cck])
                        stt("dve", cc[:, :Tn], cin[:, 0:Tn], V("cw0", ci), cc[:, :Tn], ALU.mult, ALU.add,
                            [ck, cck, "vec"], [cck])
                        cvals.append((cc, cck))
                        if mi % 2 == 1:
                            ca, cak = cvals[-2]
                            cb_, cbk = cvals[-1]
                            jl = j - 8 * r
                            act(ca[:, :Tn], ca[:, :Tn], AF.Silu, [cak], [cak])
                            tt("dve", X1[:, jl, :Tn], ca[:, :Tn], cb_[:, :Tn], ALU.mult, [cak, cbk], [("x1", jl)])
                    proj_fm(wt, wk, zb, "z", Tn, ev_up)
                nk = 8 if r < 2 else 6
                for ch in range(2):
                    wt, wk = wget(33 + 2 * r + ch)

                    def ev_dn(mi, bank, ch=ch):
                        m = 4 * ch + mi
                        tt("dve", Hbuf[:, m, :Tn], ps[:, bank, :Tn], Hbuf[:, m, :Tn], ALU.add,
                           [("ps", bank), (hkey, m)], [(hkey, m)])
                    proj_fm(wt, wk, X1, "x1", Tn, ev_dn, nk=nk)
            if is_meta:
                cp("dve", uphm, uph, [("uph", c) for c in range(44)], ["uphm"])
                stage("meta_done")
                return
            rmsnorm(Hbuf, hkey, "g3", Tn, Hbuf, hkey, X1, "x1")
            dma("sp", outT[b, :, :, t0:t0 + Tn], Hbuf[:, :, 0:Tn], sem_o[0 if hkey == "hA" else 1], [(hkey, fc) for fc in range(8)],
                [("out", b, t0)])

        tiles = [("meta", 0, 0, NMETA)] if stop_after != "prologue" else []
        for b in range(n_seq if stop_after != "prologue" else 0):
            for ti in range(n_tiles_per_seq):
                tiles.append(("main", b, ti * TT, TT))
        hkeys = ["hA", "hB"]
        if stop_after == "prologue":
            n_seq = 0
        if tiles:
            load_h(tiles[0], HB_[0], hkeys[0], 0)
        for n, tile in enumerate(tiles):
            par = n % 2
            Hbuf, hkey = HB_[par], hkeys[par]
            Ebuf, ekey = HB_[1 - par], hkeys[1 - par]
            if tile[0] == "main" and tile[2] == 0:
                cp("dve", sall[:, 0, :], smeta, ["smeta"], [("sall", 0)])
                cp("dve", st, stm, ["stm"], [("st", h) for h in range(8)])
                cp("dve", uph, uphm, ["uphm"], [("uph", c) for c in range(44)])
            nl = None
            if n + 1 < len(tiles):
                nl = (lambda n=n, par=par: load_h(tiles[n + 1], HB_[1 - par], hkeys[1 - par], 1 - par))
            try:
                do_tile(tile, Hbuf, hkey, Ebuf, ekey, nl)
            except StopBuild:
                break
        P.op("sp", lambda e: e.nop(), extra_deps=list(P.last_dma.values()))
        P.finalize(sems)
        print("instr counts", {e: len(P.streams[e]) for e in ENGS})

        @block.tensor
        def _(e):
            P.emit("pe", e)

        @block.vector
        def _(e):
            P.emit("dve", e)

        @block.scalar
        def _(e):
            P.emit("act", e)

        @block.gpsimd
        def _(e):
            P.emit("pool", e)

        @block.sync
        def _(e):
            P.emit("sp", e)
    return nc


def kernel(**inputs):
    x = np.asarray(inputs["x"], np.float32)
    packed = pack_inputs(inputs)
    nc = build_program()
    in_maps = []
    for c in range(NCORES):
        xs = x[c * SPC:(c + 1) * SPC]
        xT = np.ascontiguousarray(xs.transpose(0, 2, 1).reshape(SPC, 8, 128, SEQ).transpose(0, 2, 1, 3))
        m = dict(packed)
        m["xT"] = xT
        in_maps.append(m)
    res = run_bass_kernel_spmd(nc, in_maps, core_ids=list(range(NCORES)))
    out = np.empty((NCORES * SPC, SEQ, D), np.float32)
    for c in range(NCORES):
        oT = np.asarray(res.results[c]["outT"]).reshape(SPC, 128, 8, SEQ)
        out[c * SPC:(c + 1) * SPC] = oT.transpose(0, 3, 2, 1).reshape(SPC, SEQ, D)
    return out
```
